# Optimizing a Trainium2 kernel written in Bass

```python
import jax, jax.numpy as jnp
from jax import lax
import numpy as np

D_MODEL = 1024
BATCH = 2
SEQ = 8192
DEPTH = 2

PLE_DIM = 256
HEAD_DIM = 64
N_Q_HEADS = 16
N_KV_HEADS = 4
Q_PER_KV = N_Q_HEADS // N_KV_HEADS
WINDOW = 128
BLOCK = 128
ROPE_THETA = 10000.0
ATT_Q = N_Q_HEADS * HEAD_DIM
ATT_KV = N_KV_HEADS * HEAD_DIM
RW_HEADS = 16
RW_HEAD = 64
RW_DIM = RW_HEADS * RW_HEAD
DECAY_RANK = 64
ICLR_RANK = 64
GATE_RANK = 160
RW_SHIFT_COLS = 3 * RW_DIM + DECAY_RANK + ICLR_RANK + GATE_RANK
D_IN = ATT_Q + 2 * ATT_KV + RW_SHIFT_COLS + 2 * D_MODEL
FFN_DIM = 2816
CONV_W = 3
NORM_EPS = 1e-6
GN_EPS = 64e-5

kernel_name = 'hybrid_swa_sink_rwkv7_convffn_ple'


def _split(z, sizes):
    out, off = [], 0
    for s in sizes:
        out.append(z[..., off:off + s])
        off += s
    return out


def rmsnorm(x, g):
    xf = x.astype(jnp.float32)
    y = xf * lax.rsqrt(jnp.mean(xf * xf, axis=-1, keepdims=True) + NORM_EPS)
    return (y * g.astype(jnp.float32)).astype(x.dtype)


def rope(t, positions):
    half = HEAD_DIM // 2
    inv_freq = jnp.power(ROPE_THETA, -jnp.arange(half, dtype=jnp.float32) / half)
    ang = positions.astype(jnp.float32)[..., None] * inv_freq
    cos = jnp.cos(ang)[:, :, None, :]
    sin = jnp.sin(ang)[:, :, None, :]
    tf = t.astype(jnp.float32)
    t1, t2 = tf[..., :half], tf[..., half:]
    return jnp.concatenate([t1 * cos - t2 * sin, t2 * cos + t1 * sin], axis=-1).astype(t.dtype)


def sliding_window_attention(q, k, v, sinks):
    B, S = q.shape[0], q.shape[1]
    nb = S // BLOCK
    qb = q.astype(jnp.float32).reshape(B, nb, BLOCK, N_KV_HEADS, Q_PER_KV, HEAD_DIM)

    def kv_blocks(t):
        t = t.astype(jnp.float32)
        prev = jnp.pad(t, ((0, 0), (BLOCK, 0), (0, 0), (0, 0)))[:, :S]
        return jnp.concatenate([prev.reshape(B, nb, BLOCK, N_KV_HEADS, HEAD_DIM),
                                t.reshape(B, nb, BLOCK, N_KV_HEADS, HEAD_DIM)], axis=2)

    kb, vb = kv_blocks(k), kv_blocks(v)
    s = jnp.einsum('bnqhgd,bnkhd->bnhgqk', qb, kb) * (HEAD_DIM ** -0.5)
    qi = jnp.arange(BLOCK)[:, None]
    kj = jnp.arange(2 * BLOCK)[None, :]
    dist = qi + BLOCK - kj
    band = (dist >= 0) & (dist < WINDOW)
    kpos = jnp.arange(nb)[:, None, None] * BLOCK - BLOCK + kj[None]
    mask = band[None] & (kpos >= 0)
    s = jnp.where(mask[None, :, None, None], s, -jnp.inf)
    sink = sinks.astype(jnp.float32).reshape(N_KV_HEADS, Q_PER_KV)[None, None, :, :, None, None]
    m = jnp.maximum(jnp.max(s, axis=-1, keepdims=True), sink)
    pr = jnp.exp(s - m)
    pr = pr / (jnp.sum(pr, axis=-1, keepdims=True) + jnp.exp(sink - m))
    o = jnp.einsum('bnhgqk,bnkhd->bnqhgd', pr, vb)
    return o.reshape(B, S, ATT_Q).astype(q.dtype)


def rwkv7_scan(r, w, k, v, a, b):
    B, S, H, N = r.shape

    def step(state, inp):
        r_t, w_t, k_t, v_t, a_t, b_t = inp
        sa = jnp.einsum('bhvk,bhk->bhv', state, a_t)
        state = (state * w_t[:, :, None, :] + sa[..., None] * b_t[:, :, None, :]
                 + v_t[..., None] * k_t[:, :, None, :])
        return state, jnp.einsum('bhvk,bhk->bhv', state, r_t)

    xs = tuple(jnp.moveaxis(t.astype(jnp.float32), 1, 0) for t in (r, w, k, v, a, b))
    s0 = jnp.zeros((B, H, N, N), jnp.float32)
    _, y = lax.scan(step, s0, xs)
    return jnp.moveaxis(y, 0, 1)


def rwkv7_time_mix(z, mu, w0, w2, a0, a2, g2, k_k, k_a, r_k, gn_w, gn_b):
    B, S = z.shape[0], z.shape[1]
    zs = z + (jnp.pad(z, ((0, 0), (1, 0), (0, 0)))[:, :S] - z) * mu
    r, k, v, wd, ad, gd = _split(zs, (RW_DIM, RW_DIM, RW_DIM, DECAY_RANK, ICLR_RANK, GATE_RANK))
    w = -jax.nn.softplus(-(w0 + jnp.tanh(wd) @ w2)) - 0.5
    decay = jnp.exp(-jnp.exp(w.astype(jnp.float32)))
    iclr = jax.nn.sigmoid(a0 + ad @ a2)
    g = jax.nn.sigmoid(gd) @ g2

    def heads(t):
        return t.astype(jnp.float32).reshape(B, S, RW_HEADS, RW_HEAD)

    kk = heads(k * k_k)
    kk = kk / jnp.maximum(jnp.sqrt(jnp.sum(kk * kk, axis=-1, keepdims=True)), 1e-12)
    k = k * (1.0 + (iclr - 1.0) * k_a)
    rh, kh, vh, ah = heads(r), heads(k), heads(v), heads(iclr)
    y = rwkv7_scan(rh, heads(decay), kh, vh, -kk, kk * ah)
    mean = jnp.mean(y, axis=-1, keepdims=True)
    var = jnp.mean(jnp.square(y - mean), axis=-1, keepdims=True)
    y = ((y - mean) * lax.rsqrt(var + GN_EPS)).reshape(B, S, RW_DIM) * gn_w + gn_b
    bonus = jnp.sum(rh * kh * r_k.astype(jnp.float32), axis=-1, keepdims=True) * vh
    out = (y + bonus.reshape(B, S, RW_DIM)) * g
    return out.astype(z.dtype)


def causal_dwconv(u, w, b):
    C = u.shape[-1]
    y = lax.conv_general_dilated(u, w[:, None, :].astype(u.dtype), window_strides=(1,),
                                 padding=[(CONV_W - 1, 0)],
                                 dimension_numbers=('NWC', 'WIO', 'NWC'),
                                 feature_group_count=C)
    return y + b


def hybrid_layer(x, p_i, positions, norm_mix_pre, w_in, att_sinks, w_o_att, rw_mu, rw_w0,
                 rw_w2, rw_a0, rw_a2, rw_g2, rw_k_k, rw_k_a, rw_r_k, rw_gn_w, rw_gn_b,
                 w_o_rw, w_out, norm_mix_post, norm_ffn_pre, w_up, conv_w, conv_b, w_down,
                 norm_ffn_post, w_ple, w_ple_gate, norm_ple):
    B, S = x.shape[0], x.shape[1]
    h = rmsnorm(x, norm_mix_pre)
    proj = h @ w_in
    q, k, v, rw, gate_a, gate_b = _split(proj, (ATT_Q, ATT_KV, ATT_KV, RW_SHIFT_COLS, D_MODEL, D_MODEL))
    q = rope(q.reshape(B, S, N_Q_HEADS, HEAD_DIM), positions)
    k = rope(k.reshape(B, S, N_KV_HEADS, HEAD_DIM), positions)
    v = v.reshape(B, S, N_KV_HEADS, HEAD_DIM)
    y_att = sliding_window_attention(q, k, v, att_sinks) @ w_o_att
    y_rw = rwkv7_time_mix(rw, rw_mu, rw_w0, rw_w2, rw_a0, rw_a2, rw_g2, rw_k_k, rw_k_a,
                          rw_r_k, rw_gn_w, rw_gn_b) @ w_o_rw
    mixed = jax.nn.sigmoid(gate_a) * y_att + jax.nn.sigmoid(gate_b) * y_rw
    x = x + rmsnorm(mixed @ w_out, norm_mix_post)
    h = rmsnorm(x, norm_ffn_pre)
    u = causal_dwconv(h @ w_up, conv_w, conv_b)
    ua, ub = _split(u, (FFN_DIM, FFN_DIM))
    x = x + rmsnorm((jax.nn.gelu(ua) * ub) @ w_down, norm_ffn_post)
    e = p_i @ w_ple
    gate = jax.nn.sigmoid(x @ w_ple_gate)
    x = x + rmsnorm(gate * e, norm_ple)
    return x


def setup_inputs(seed: int = 0) -> dict:
    key = jax.random.key(seed)
    ks = iter(jax.random.split(key, 40))

    def nrm(shape, scale):
        return jax.random.normal(next(ks), shape, jnp.float32) * scale

    def gain(n):
        return 1.0 + nrm((DEPTH, n), 0.05)

    L = DEPTH
    return {
        'x': nrm((BATCH, SEQ, D_MODEL), 1.0),
        'p': nrm((DEPTH, BATCH, SEQ, PLE_DIM), 1.0),
        'positions': jnp.broadcast_to(jnp.arange(SEQ, dtype=jnp.int32)[None], (BATCH, SEQ)),
        'norm_mix_pre': gain(D_MODEL),
        'w_in': nrm((L, D_MODEL, D_IN), D_MODEL ** -0.5),
        'att_sinks': nrm((L, N_Q_HEADS), 0.5),
        'w_o_att': nrm((L, ATT_Q, D_MODEL), ATT_Q ** -0.5),
        'rw_mu': jax.random.uniform(next(ks), (L, RW_SHIFT_COLS), jnp.float32),
        'rw_w0': nrm((L, RW_DIM), 0.5),
        'rw_w2': nrm((L, DECAY_RANK, RW_DIM), DECAY_RANK ** -0.5),
        'rw_a0': nrm((L, RW_DIM), 0.5),
        'rw_a2': nrm((L, ICLR_RANK, RW_DIM), ICLR_RANK ** -0.5),
        'rw_g2': nrm((L, GATE_RANK, RW_DIM), GATE_RANK ** -0.5),
        'rw_k_k': 0.85 + nrm((L, RW_DIM), 0.1),
        'rw_k_a': 1.0 + nrm((L, RW_DIM), 0.1),
        'rw_r_k': nrm((L, RW_HEADS, RW_HEAD), 0.1),
        'rw_gn_w': gain(RW_DIM),
        'rw_gn_b': nrm((L, RW_DIM), 0.01),
        'w_o_rw': nrm((L, RW_DIM, D_MODEL), RW_DIM ** -0.5),
        'w_out': nrm((L, D_MODEL, D_MODEL), D_MODEL ** -0.5),
        'norm_mix_post': gain(D_MODEL),
        'norm_ffn_pre': gain(D_MODEL),
        'w_up': nrm((L, D_MODEL, 2 * FFN_DIM), D_MODEL ** -0.5),
        'conv_w': nrm((L, CONV_W, 2 * FFN_DIM), CONV_W ** -0.5),
        'conv_b': nrm((L, 2 * FFN_DIM), 0.01),
        'w_down': nrm((L, FFN_DIM, D_MODEL), FFN_DIM ** -0.5),
        'norm_ffn_post': gain(D_MODEL),
        'w_ple': nrm((L, PLE_DIM, D_MODEL), PLE_DIM ** -0.5),
        'w_ple_gate': nrm((L, D_MODEL, D_MODEL), D_MODEL ** -0.5),
        'norm_ple': gain(D_MODEL),
    }


def reference(x, p, positions, norm_mix_pre, w_in, att_sinks, w_o_att, rw_mu, rw_w0, rw_w2,
              rw_a0, rw_a2, rw_g2, rw_k_k, rw_k_a, rw_r_k, rw_gn_w, rw_gn_b, w_o_rw, w_out,
              norm_mix_post, norm_ffn_pre, w_up, conv_w, conv_b, w_down, norm_ffn_post,
              w_ple, w_ple_gate, norm_ple):
    for i in range(DEPTH):
        x = hybrid_layer(x, p[i], positions, norm_mix_pre[i], w_in[i], att_sinks[i], w_o_att[i],
                         rw_mu[i], rw_w0[i], rw_w2[i], rw_a0[i], rw_a2[i], rw_g2[i], rw_k_k[i],
                         rw_k_a[i], rw_r_k[i], rw_gn_w[i], rw_gn_b[i], w_o_rw[i], w_out[i],
                         norm_mix_post[i], norm_ffn_pre[i], w_up[i], conv_w[i], conv_b[i],
                         w_down[i], norm_ffn_post[i], w_ple[i], w_ple_gate[i], norm_ple[i])
    return x
```

```python
import numpy as np
from contextlib import ExitStack
import concourse.bass as bass
import concourse.mybir as mybir
from concourse.bass_utils import run_bass_kernel_spmd

F32 = mybir.dt.float32
BF16 = mybir.dt.bfloat16
I32 = mybir.dt.int32
AF = mybir.ActivationFunctionType
ALU = mybir.AluOpType
AX = mybir.AxisListType

D = 1024
NL = 2
FFN = 2816
SEG = 8000
DMA_LIM = 48000
SAME_ENGINE_SYNC = True
NSLOT = 3
C0 = float(np.exp(-0.5))
TWO_PI = 2.0 * np.pi
STOP = None


class _Stop(Exception):
    pass


def chk(n):
    if STOP is not None and n == STOP:
        raise _Stop()


class Buf:
    __slots__ = ('w', 'r', 'name', 'dsem', 'dcnt', 'dlast', 'rng')

    def __init__(self, name='', rng=None):
        self.w = None
        self.r = {}
        self.name = name
        self.dsem = None
        self.dcnt = 0
        self.dlast = None
        self.rng = rng


class Prog:
    ENGS = ('pe', 'act', 'dve', 'pool', 'sp')

    def __init__(self, nc, stack):
        self.nc = nc
        self.stack = stack
        self.streams = {e: [] for e in self.ENGS}
        self.cnt = {e: 0 for e in self.ENGS}
        self.sems = {e: [] for e in self.ENGS}
        self.seen = {e: {} for e in self.ENGS}
        self.nsem = 0
        self.arena_bufs = []

    def new_sem(self, name):
        self.nsem += 1
        return self.stack.enter_context(self.nc.semaphore(f"s{self.nsem}_{name}"))

    def _tok(self, eng):
        k = self.cnt[eng]
        self.cnt[eng] = k + 1
        seg = k // SEG
        while len(self.sems[eng]) <= seg:
            self.sems[eng].append(self.new_sem(eng))
        return (self.sems[eng][seg], (k % SEG) + 1, eng)

    def _expand(self, bufs):
        out = []
        for b in bufs:
            out.append(b)
            if b.rng is not None:
                s, e = b.rng
                for o in self.arena_bufs:
                    if o is not b and o.rng[0] < e and s < o.rng[1]:
                        out.append(o)
        return out

    def _collect(self, eng, reads, writes):
        waits = {}

        def add(t):
            if t is None:
                return
            s, v, src = t
            if src == eng and (eng == 'pe' or not SAME_ENGINE_SYNC):
                return
            key = id(s)
            if self.seen[eng].get(key, 0) >= v:
                return
            if key not in waits or waits[key][1] < v:
                waits[key] = (s, v)
        for b in self._expand(reads):
            add(b.w)
        for b in self._expand(writes):
            add(b.w)
            for t in b.r.values():
                add(t)
        for key, (s, v) in waits.items():
            self.seen[eng][key] = v
        return list(waits.values())

    def _commit(self, tok, reads, writes):
        key = id(tok[0])
        for b in reads:
            b.r[key] = tok
        for b in writes:
            b.w = tok
            b.r = {}

    def op(self, eng, fn, reads=(), writes=()):
        waits = self._collect(eng, reads, writes)
        tok = self._tok(eng)
        self.streams[eng].append((waits, fn, tok[0], 1))
        self._commit(tok, reads, writes)
        return tok

    def mm(self, fns, reads=(), writes=()):
        waits = self._collect('pe', reads, writes)
        for f in fns[:-1]:
            self.streams['pe'].append((waits, f, None, 0))
            waits = []
        tok = self._tok('pe')
        self.streams['pe'].append((waits, fns[-1], tok[0], 1))
        self._commit(tok, reads, writes)
        return tok

    def dma(self, eng, fn, owner, reads=(), writes=()):
        if owner.dsem is None or owner.dcnt + 16 > DMA_LIM:
            owner.dsem = self.new_sem('d' + owner.name)
            owner.dcnt = 0
            owner.dlast = None
        waits = self._collect(eng, reads, writes)
        if owner.dlast is not None:
            s, v, _ = owner.dlast
            if self.seen[eng].get(id(s), 0) < v:
                self.seen[eng][id(s)] = v
                waits.append((s, v))
        owner.dcnt += 16
        tok = (owner.dsem, owner.dcnt, 'dma')
        owner.dlast = tok
        self.streams[eng].append((waits, fn, tok[0], 16))
        self._commit(tok, reads, writes)
        return tok

    def wait_all(self, eng, bufs):
        waits = self._collect(eng, bufs, bufs)
        self.streams[eng].append((waits, None, None, 0))

    def emit(self):
        nc = self.nc
        engs = {'pe': 'tensor', 'act': 'scalar', 'dve': 'vector', 'pool': 'gpsimd', 'sp': 'sync'}
        with nc.Block() as block:
            for k, attr in engs.items():
                stream = self.streams[k]

                def body(e, stream=stream):
                    for waits, fn, sem, inc in stream:
                        for s, v in waits:
                            e.wait_ge(s, v)
                        if fn is None:
                            continue
                        inst = fn(e)
                        if sem is not None:
                            inst.then_inc(sem, inc)
                getattr(block, attr)(body)


def MM(out, lhsT, rhs, start=True, stop=True):
    return lambda e: e.matmul(out, lhsT=lhsT, rhs=rhs, start=start, stop=stop)


def TR(out, in_, ident):
    return lambda e: e.transpose(out=out, in_=in_, identity=ident)


def ACTF(out, in_, func, scale=None, bias=None):
    kw = {}
    if scale is not None:
        kw['scale'] = scale
    if bias is not None:
        kw['bias'] = bias
    return lambda e: e.activation(out=out, in_=in_, func=func, **kw)


def TT(out, in0, in1, op):
    return lambda e: e.tensor_tensor(out=out, in0=in0, in1=in1, op=op)


def TS(out, in0, s1, op0, s2=None, op1=None):
    if op1 is None:
        return lambda e: e.tensor_scalar(out=out, in0=in0, scalar1=s1, scalar2=None, op0=op0)
    return lambda e: e.tensor_scalar(out=out, in0=in0, scalar1=s1, scalar2=s2, op0=op0, op1=op1)


def STT(out, in0, scalar, in1, op0, op1):
    return lambda e: e.scalar_tensor_tensor(out=out, in0=in0, scalar=scalar, in1=in1, op0=op0, op1=op1)


def CP(out, in_):
    return lambda e: e.tensor_copy(out=out, in_=in_)


def RSUM(out, in_):
    return lambda e: e.tensor_reduce(out=out, in_=in_, axis=AX.X, op=ALU.add)


def RECIP(out, in_):
    return lambda e: e.reciprocal(out=out, in_=in_)


def MSET(ap, v):
    return lambda e: e.memset(ap, v)


def DMA(out, in_):
    return lambda e: e.dma_start(out=out, in_=in_)


def _blk_proj(W, cols):
    sub = np.ascontiguousarray(W[:, cols])
    return sub.reshape(8, 128, -1).transpose(1, 0, 2).reshape(128, -1)


def _layer_blocks(inp, l):
    W_in = inp['w_in'][l]
    blocks = []
    for cb in range(3):
        blocks.append((f"qkv{cb}", _blk_proj(W_in, np.arange(cb * 512, (cb + 1) * 512))))
    for rb in range(6):
        blocks.append((f"rw{rb}", _blk_proj(W_in, 1536 + np.arange(rb * 512, (rb + 1) * 512))))
    blocks.append(("lora1", _blk_proj(W_in, np.arange(4608, 4896))))
    l2 = np.zeros((128, 3, 1024), np.float32)
    l2[0:64, 0] = inp['rw_w2'][l]
    l2[64:128, 0] = inp['rw_a2'][l]
    l2[:, 1] = inp['rw_g2'][l][0:128]
    l2[0:32, 2] = inp['rw_g2'][l][128:160]
    blocks.append(("lora2", l2.reshape(128, -1)))
    for br, (goff, wo) in enumerate(((4896, inp['w_o_att'][l]), (5920, inp['w_o_rw'][l]))):
        for cb in range(2):
            blocks.append((f"g{br}{cb}", _blk_proj(W_in, goff + np.arange(cb * 512, (cb + 1) * 512))))
            blocks.append((f"wo{br}{cb}", _blk_proj(wo, np.arange(cb * 512, (cb + 1) * 512))))
    for cb in range(2):
        blocks.append((f"wout{cb}", _blk_proj(inp['w_out'][l], np.arange(cb * 512, (cb + 1) * 512))))
    W_up = inp['w_up'][l]
    for i in range(11):
        cols = []
        for c in (2 * i, 2 * i + 1):
            cols.append(np.arange(c * 128, (c + 1) * 128))
            cols.append(FFN + np.arange(c * 128, (c + 1) * 128))
        blocks.append((f"up{i}", _blk_proj(W_up, np.concatenate(cols))))
    Wd = inp['w_down'][l]
    for db in range(6):
        nj = 4 if db < 5 else 2
        a = Wd[db * 512: db * 512 + nj * 128].reshape(nj, 128, 1024).transpose(1, 0, 2).reshape(128, -1)
        blocks.append((f"down{db}", a))
    blocks.append(("ple", inp['w_ple'][l].reshape(2, 128, 1024).transpose(1, 0, 2).reshape(128, -1)))
    for cb in range(2):
        blocks.append((f"pg{cb}", _blk_proj(inp['w_ple_gate'][l], np.arange(cb * 512, (cb + 1) * 512))))
    return blocks


NPF = 27 + 9 * 8 + 132 + 44
PF_MU, PF_W0, PF_A0, PF_KK, PF_KA, PF_GNW, PF_GNB, PF_RK, PF_GMIX, PF_GFFN, PF_CW, PF_CB = (
    0, 27, 35, 43, 51, 59, 67, 75, 83, 91, 99, 231)
NPT = 3 * 1024 + 16


def _layer_pfm(inp, l):
    a = np.zeros((128, NPF), np.float32)
    mu = inp['rw_mu'][l]
    a[:, 0:26] = mu[0:26 * 128].reshape(26, 128).T
    a[0:32, 26] = mu[26 * 128:]
    for off, key in ((PF_W0, 'rw_w0'), (PF_A0, 'rw_a0'), (PF_KK, 'rw_k_k'), (PF_KA, 'rw_k_a'),
                     (PF_GNW, 'rw_gn_w'), (PF_GNB, 'rw_gn_b'), (PF_RK, 'rw_r_k'),
                     (PF_GMIX, 'norm_mix_pre'), (PF_GFFN, 'norm_ffn_pre')):
        a[:, off:off + 8] = inp[key][l].reshape(8, 128).T
    cw = inp['conv_w'][l]
    a[:, PF_CW:PF_CW + 132] = cw.reshape(3, 44, 128).transpose(2, 0, 1).reshape(128, 132)
    a[:, PF_CB:PF_CB + 44] = inp['conv_b'][l].reshape(44, 128).T
    return a


def _layer_ptm(inp, l):
    row = np.concatenate([inp['norm_mix_post'][l], inp['norm_ffn_post'][l], inp['norm_ple'][l],
                          inp['att_sinks'][l]]).astype(np.float32)
    return np.ascontiguousarray(np.broadcast_to(row[None, :], (128, NPT)))


CS_ID, CS_MU, CS_MSU, CS_MSL, CS_ONES, CS_IF, CS_SCAN = 0, 128, 256, 384, 512, 640, 672
NCST = 672 + 1024


def _consts():
    c = np.zeros((128, NCST), np.float32)
    r = np.arange(128)[:, None]
    q = np.arange(128)[None, :]
    c[:, CS_ID:CS_ID + 128] = np.eye(128)
    c[:, CS_MU:CS_MU + 128] = (r <= q)
    c[:, CS_MSU:CS_MSU + 128] = (r < q)
    c[:, CS_MSL:CS_MSL + 128] = (r > q)
    c[:, CS_ONES:CS_ONES + 128] = ((r // 64) == (q // 64))
    half = 32
    c[:, CS_IF:CS_IF + 32] = np.power(np.float32(10000.0), -np.arange(half, dtype=np.float32) / half)[None, :]
    sm = np.ones(1024, np.float32)
    sm[::128] = 0.0
    c[:, CS_SCAN:CS_SCAN + 1024] = sm[None, :]
    return c


def build(S, L, blocks_meta, TOT):
    NT = S // 128
    NB = len(blocks_meta)
    nc = bass.Bass("TRN2", target_bir_lowering=False)
    x_d = nc.dram_tensor("x", [S, D], F32, kind="ExternalInput").ap()
    pT_d = nc.dram_tensor("pT", [128, L, 2, S], F32, kind="ExternalInput").ap()
    pos_d = nc.dram_tensor("pos", [S, 1], I32, kind="ExternalInput").ap()
    w_d = nc.dram_tensor("wcat", [L, 128, TOT], F32, kind="ExternalInput").ap()
    pfm_d = nc.dram_tensor("pfm", [128, L, NPF], F32, kind="ExternalInput").ap()
    ptm_d = nc.dram_tensor("ptm", [L, 128, NPT], F32, kind="ExternalInput").ap()
    cst_d = nc.dram_tensor("cst", [128, NCST], F32, kind="ExternalInput").ap()
    wb_d = nc.dram_tensor("wbf", [L, 128, TOT], BF16, kind="Internal").ap()
    out_d = nc.dram_tensor("out", [S, D], F32, kind="ExternalOutput").ap()

    with ExitStack() as st:
        P = Prog(nc, st)

        def sb(name, shape, dt):
            return st.enter_context(nc.sbuf_tensor("s_" + name, shape, dt)), Buf(name)

        ring, b_ring = [], []
        for i in range(NSLOT):
            t_, b_ = sb(f"ring{i}", [128, 4096], BF16)
            ring.append(t_)
            b_ring.append(b_)
        xt, b_x = sb("xt", [128, D], F32)
        cstf, b_cstf = sb("cstf", [128, NCST], F32)
        cstb, b_cstb = sb("cstb", [128, 640], BF16)
        scanm, b_scanm = sb("scanm", [128, 1024], BF16)
        pfm, b_pfm = sb("pfm", [128, L, NPF], F32)
        ptm, b_ptm = sb("ptm", [128, NPT], F32)
        esink, b_esink = sb("esink", [128, 16], F32)
        hT, b_hT = sb("hT", [128, 8, 128], BF16)
        kTr, b_kTr = [], []
        vr, b_vr = [], []
        for l in range(L):
            for j in range(2):
                t_, b_ = sb(f"kT{l}{j}", [64, 4, 128], BF16)
                kTr.append(t_); b_kTr.append(b_)
                t_, b_ = sb(f"v{l}{j}", [128, 4, 65], BF16)
                vr.append(t_); b_vr.append(b_)
        zcar, b_zcar = [], []
        ccar, b_ccar = [], []
        Hst, b_H = [], []
        for l in range(L):
            t_, b_ = sb(f"zcar{l}", [128, 27], F32); zcar.append(t_); b_zcar.append(b_)
            t_, b_ = sb(f"ccar{l}", [128, 44, 2], F32); ccar.append(t_); b_ccar.append(b_)
            t_, b_ = sb(f"H{l}", [128, 8, 64], F32); Hst.append(t_); b_H.append(b_)
        cos8, b_cos8 = sb("cos8", [128, 8, 32], F32)
        sin8, b_sin8 = sb("sin8", [128, 8, 32], F32)
        rtmp, b_rtmp = sb("rtmp", [128, 6, 32], F32)
        posi, b_posi = sb("posi", [128, 1], I32)
        small, b_small = sb("small", [128, 64], F32)
        pTf, b_pTf = sb("pTf", [128, L, 2, 128], F32)
        pTb, b_pTb = sb("pTb", [128, L, 2, 128], BF16)

        ARW = 26 * 1024
        arena = st.enter_context(nc.sbuf_tensor("arena", [128, ARW], F32))
        ptr = {}

        def aal(phase, name, words, dt=F32):
            o = ptr.get(phase, 0)
            ptr[phase] = o + words
            assert ptr[phase] <= ARW, (phase, name, ptr[phase])
            b = Buf(name, rng=(o, o + words))
            P.arena_bufs.append(b)
            v = arena[:, o:o + words]
            if dt == BF16:
                v = v.bitcast(BF16)
            return v, b

        def common(phase):
            d = {}
            d['scr'], d['b_scr'] = aal(phase, "scr", 1024)
            d['hn'], d['b_hn'] = aal(phase, "hn", 512, BF16)
            return d
        cm = common('all')
        for ph in ('att', 'rw', 'ffn', 'mrg'):
            ptr[ph] = ptr['all']
        scr, b_scr, hn, b_hn = cm['scr'], cm['b_scr'], cm['hn'], cm['b_hn']

        pst = [st.enter_context(nc.psum_tensor(f"ps{i}", [128, 1024], F32)) for i in range(4)]
        b_bank = [Buf(f"bank{i}") for i in range(8)]

        def bank(i):
            return pst[i // 2][:, (i % 2) * 512:(i % 2 + 1) * 512]

        def bankb(i):
            return bank(i).bitcast(BF16)

        identb = cstb[:, 0:128]
        mask_u = cstb[:, 128:256]
        mask_su = cstb[:, 256:384]
        mask_sl = cstb[:, 384:512]
        identf = cstf[:, CS_ID:CS_ID + 128]
        onesf = cstf[:, CS_ONES:CS_ONES + 128]
        invf = cstf[:, CS_IF:CS_IF + 32]

        P.dma('sp', DMA(cstf[:], cst_d[:, :]), b_cstf, writes=[b_cstf])
        P.dma('sp', DMA(pfm[:], pfm_d[:, :, :]), b_pfm, writes=[b_pfm])
        P.op('dve', CP(cstb[:], cstf[:, 0:640]), reads=[b_cstf], writes=[b_cstb])
        P.op('dve', CP(scanm[:], cstf[:, CS_SCAN:CS_SCAN + 1024]), reads=[b_cstf], writes=[b_scanm])
        for l in range(L):
            P.op('pool', MSET(zcar[l][:], 0.0), writes=[b_zcar[l]])
            P.op('pool', MSET(ccar[l][:], 0.0), writes=[b_ccar[l]])
            P.op('pool', MSET(Hst[l][:], 0.0), writes=[b_H[l]])
            for j in range(2):
                P.op('pool', MSET(vr[l * 2 + j][:], 1.0), writes=[b_vr[l * 2 + j]])
        b_wblk = [[Buf(f"wb{l}_{i}") for i in range(NB)] for l in range(L)]
        castown = [Buf(f"cast{i}") for i in range(8)]
        for l in range(L):
            for i, (name, off, n) in enumerate(blocks_meta):
                pc = 1024 if n % 1024 == 0 else 256
                src = w_d[l, :, off:off + n].rearrange("p (a b) -> p a b", b=pc)
                dst = wb_d[l, :, off:off + n].rearrange("p (a b) -> p a b", b=pc)
                flat = l * NB + i
                P.dma('pool', DMA(dst, src), castown[flat % 8], writes=[b_wblk[l][i]])
                if flat >= 6:
                    pl, pi = divmod(flat - 6, NB)
                    P.wait_all('pool', [b_wblk[pl][pi]])

        seq = [(t, l, i) for t in range(NT) for l in range(L) for i in range(NB)]
        state = {'issued': 0, 'next': 0}

        def issue_upto(q):
            while state['issued'] <= q and state['issued'] < len(seq):
                k = state['issued']
                t, l, i = seq[k]
                name, off, n = blocks_meta[i]
                slot = k % NSLOT
                P.dma('sp', DMA(ring[slot][:, 0:n], wb_d[l, :, off:off + n]), b_ring[slot],
                      reads=[b_wblk[l][i]], writes=[b_ring[slot]])
                state['issued'] = k + 1

        def wget(t, l, name):
            k = state['next']
            assert seq[k][0] == t and seq[k][1] == l and blocks_meta[seq[k][2]][0] == name, (seq[k], name)
            issue_upto(k + NSLOT - 1)
            state['next'] = k + 1
            slot = k % NSLOT
            return ring[slot], b_ring[slot]

        def rms_rstd(src_ap, src_bufs, out_col, eps=1e-6, n=1024.0):
            if isinstance(src_ap, list):
                for hI, a in enumerate(src_ap):
                    P.op('act', ACTF(scr[:, hI * 512:(hI + 1) * 512], a, AF.Square), reads=[src_bufs[hI]], writes=[b_scr])
            else:
                P.op('act', ACTF(scr[:, :], src_ap, AF.Square), reads=src_bufs, writes=[b_scr])
            P.op('dve', RSUM(small[:, out_col:out_col + 1], scr[:, :]), reads=[b_scr], writes=[b_small])
            P.op('act', ACTF(small[:, out_col:out_col + 1], small[:, out_col:out_col + 1], AF.Sqrt, scale=1.0 / n, bias=eps),
                 reads=[b_small], writes=[b_small])
            P.op('dve', RECIP(small[:, out_col:out_col + 1], small[:, out_col:out_col + 1]), reads=[b_small], writes=[b_small])

        def pre_norm_T(l, goff):
            rms_rstd(xt[:, :], [b_x], 0)
            P.op('dve', TS(hn[:, :], xt[:, :], small[:, 0:1], ALU.mult), reads=[b_x, b_small], writes=[b_hn])
            P.mm([TR(bankb(0)[:, j * 128:(j + 1) * 128], hn[:, j * 128:(j + 1) * 128], identb) for j in range(8)],
                 reads=[b_hn, b_cstb], writes=[b_bank[0]])
            g = pfm[:, l, goff:goff + 8].unsqueeze(2).to_broadcast([128, 8, 128])
            P.op('dve', TT(hT[:, :, :], bankb(0)[:, :].rearrange("p (a b) -> p a b", a=8), g, ALU.mult),
                 reads=[b_bank[0], b_pfm], writes=[b_hT])

        def post_norm_add(srcs, src_bufs, gcol):
            rms_rstd(srcs, src_bufs, 1)
            for hI in range(2):
                P.op('dve', STT(scr[:, hI * 512:(hI + 1) * 512], srcs[hI], small[:, 1:2],
                                ptm[:, gcol + hI * 512: gcol + (hI + 1) * 512], ALU.mult, ALU.mult),
                     reads=[src_bufs[hI], b_small, b_ptm], writes=[b_scr])
            P.op('pool', TT(xt[:, :], xt[:, :], scr[:, :], ALU.add), reads=[b_scr, b_x], writes=[b_x])

        def proj_tok(t, l, name, lhs, b_lhs, bk, K=8):
            blk, b_blk = wget(t, l, name)
            bv = blk[:, :].rearrange("p (a b) -> p a b", a=K)
            P.mm([MM(bank(bk), lhs[:, j, :], bv[:, j, :], start=(j == 0), stop=(j == K - 1)) for j in range(K)],
                 reads=[b_lhs, b_blk], writes=[b_bank[bk]])

        qr, b_qr = aal('att', "qr", 640, BF16)
        qT, b_qT = aal('att', "qT", 1024, BF16)
        pex, b_pex = [], []
        pm, b_pm = [], []
        for j in range(2):
            a_, b_ = aal('att', f"pex{j}", 256, BF16); pex.append(a_); b_pex.append(b_)
            a_, b_ = aal('att', f"pm{j}", 256, BF16); pm.append(a_); b_pm.append(b_)
        atok, b_atok = aal('att', "atok", 512, BF16)
        den, b_den = aal('att', "den", 16)
        AT, b_AT = sb("AT", [128, 8, 128], BF16)
        BT, b_BT = sb("BT", [128, 8, 128], BF16)
        zbuf, b_zbuf = aal('rw', "zbuf", 27 * 129)
        zs, b_zs = aal('rw', "zs", 27 * 128)
        NF = 6
        Fm, b_F = [], []
        for i in range(NF):
            a_, b_ = aal('rw', f"F{i}", 1024); Fm.append(a_); b_F.append(b_)
        la, b_la = aal('rw', "la", 256, BF16)
        ARt, b_AR = aal('rw', "AR", 1024, BF16)
        Btt, b_Bt = aal('rw', "Bt", 512, BF16)
        Ktt, b_Kt = aal('rw', "Kt", 512, BF16)
        Vtt, b_Vt = aal('rw', "Vt", 512, BF16)
        Btok, b_Btok = aal('rw', "Btok", 512, BF16)
        Ktok, b_Ktok = aal('rw', "Ktok", 512, BF16)
        Vtok, b_Vtok = aal('rw', "Vtok", 512, BF16)
        Hmid, b_Hmid = aal('rw', "Hmbd", 512, BF16)
        ARbd, b_ARbd = aal('rw', "ARbd", 2048, BF16)
        Hdec, b_Hdec = aal('rw', "Hdec", 512)
        ecol, b_ecol = aal('rw', "ecol", 24)
        MA, b_MA = aal('rw', "MA", 512, BF16)
        KA, b_KA = aal('rw', "KA", 512, BF16)
        Mf, b_Mf, Lf, b_Lf = [], [], [], []
        for j in range(2):
            a_, b_ = aal('rw', f"Mf{j}", 512); Mf.append(a_); b_Mf.append(b_)
            a_, b_ = aal('rw', f"Lf{j}", 512); Lf.append(a_); b_Lf.append(b_)
        Uf, b_Uf = [], []
        for j in range(2):
            a_, b_ = aal('rw', f"Uf{j}", 256); Uf.append(a_); b_Uf.append(b_)
        Ub, b_Ub = aal('rw', "Ub", 512, BF16)
        gst, b_gst = aal('rw', "gst", 64)
        ubuf, b_ubuf = aal('ffn', "ubuf", 4 * 130)
        cacc, b_cacc = aal('ffn', "cacc", 512)
        gel, b_gel = aal('ffn', "gel", 256)
        gmT, b_gmT = aal('ffn', "gmT", 22 * 64, BF16)
        sig, b_sig = [], []
        for j in range(2):
            a_, b_ = aal('mrg', f"sig{j}", 512); sig.append(a_); b_sig.append(b_)
        mixed, b_mixed = aal('mrg', "mixed", 1024)
        mixb, b_mixb = aal('mrg', "mixb", 512, BF16)
        mT, b_mT = aal('mrg', "mT", 512, BF16)

        qr3 = qr.rearrange("p (h d) -> p h d", h=20)
        qT3 = qT.rearrange("p (h t) -> p h t", h=16)
        atok3 = atok.rearrange("p (h d) -> p h d", h=16)
        zb3 = zbuf.rearrange("p (c t) -> p c t", c=27)
        zs3 = zs.rearrange("p (c t) -> p c t", c=27)
        F3 = [f.rearrange("p (a t) -> p a t", a=8) for f in Fm]
        la3 = la.rearrange("p (a t) -> p a t", a=4)
        AR4 = ARt.rearrange("p (a b t) -> p a b t", a=8, b=2)
        Bt3 = Btt.rearrange("p (a t) -> p a t", a=8)
        Kt3 = Ktt.rearrange("p (a t) -> p a t", a=8)
        Vt3 = Vtt.rearrange("p (a t) -> p a t", a=8)
        Hmbd4 = Hmid.rearrange("p (a h v) -> p a h v", a=8, h=2)
        ARbd5 = ARbd.rearrange("p (a h b t) -> p a h b t", a=8, h=2, b=2)
        Hdec3 = Hdec.rearrange("p (a v) -> p a v", a=8)
        MA4 = MA.rearrange("p (h b t) -> p h b t", h=4, b=2)
        KA4 = KA.rearrange("p (h b t) -> p h b t", h=4, b=2)
        Mf3 = [m.rearrange("p (h t) -> p h t", h=4) for m in Mf]
        Lf3 = [m.rearrange("p (h t) -> p h t", h=4) for m in Lf]
        Uf3 = [u.rearrange("p (h v) -> p h v", h=4) for u in Uf]
        Ub3 = Ub.rearrange("p (h v) -> p h v", h=16)
        ub3 = ubuf.rearrange("p (c t) -> p c t", c=4)
        cacc3 = cacc.rearrange("p (c t) -> p c t", c=4)
        gel3 = gel.rearrange("p (c t) -> p c t", c=2)
        gmT3 = gmT.rearrange("p (c t) -> p c t", c=22)
        mT3 = mT.rearrange("p (a t) -> p a t", a=8)

        def bc(ap, shape):
            return ap.to_broadcast(shape)

        def pcol(l, off, n=8):
            return pfm[:, l, off:off + n]

        try:
          for t in range(NT):
              tok0 = t * 128
              P.dma('sp', DMA(xt[:, :], x_d[tok0:tok0 + 128, :]), b_x, writes=[b_x])
              P.dma('sp', DMA(pTf[:, :, :, :], pT_d[:, :, :, tok0:tok0 + 128]), b_pTf, writes=[b_pTf])
              P.dma('sp', DMA(posi[:, :], pos_d[tok0:tok0 + 128, :]), b_posi, writes=[b_posi])
              P.op('pool', CP(pTb[:, :, :, :], pTf[:, :, :, :]), reads=[b_pTf], writes=[b_pTb])
              posf = small[:, 8:9]
              P.op('dve', CP(posf, posi[:, :]), reads=[b_posi], writes=[b_small])
              P.op('dve', TS(rtmp[:, 1, :], invf, posf, ALU.mult), reads=[b_cstf, b_small], writes=[b_rtmp])
              for which, shift, dst, b_dst in ((0, 0.0, sin8, b_sin8), (1, np.pi / 2, cos8, b_cos8)):
                  a_in = rtmp[:, 1, :]
                  if shift != 0.0:
                      P.op('dve', TS(rtmp[:, 4, :], rtmp[:, 1, :], float(shift), ALU.add), reads=[b_rtmp], writes=[b_rtmp])
                      a_in = rtmp[:, 4, :]
                  P.op('dve', TS(rtmp[:, 2, :], a_in, float(1.0 / TWO_PI), ALU.mult), reads=[b_rtmp], writes=[b_rtmp])
                  ki = rtmp[:, 5, :].bitcast(I32)
                  P.op('dve', CP(ki, rtmp[:, 2, :]), reads=[b_rtmp], writes=[b_rtmp])
                  P.op('dve', CP(rtmp[:, 2, :], ki), reads=[b_rtmp], writes=[b_rtmp])
                  P.op('dve', STT(rtmp[:, 3, :], rtmp[:, 2, :], -6.28125, a_in, ALU.mult, ALU.add), reads=[b_rtmp], writes=[b_rtmp])
                  P.op('dve', STT(rtmp[:, 3, :], rtmp[:, 2, :], float(-(TWO_PI - 6.28125)), rtmp[:, 3, :], ALU.mult, ALU.add),
                       reads=[b_rtmp], writes=[b_rtmp])
                  P.op('dve', TS(rtmp[:, 2, :], rtmp[:, 3, :], float(np.pi), ALU.is_gt, float(-TWO_PI), ALU.mult),
                       reads=[b_rtmp], writes=[b_rtmp])
                  P.op('dve', TT(rtmp[:, 3, :], rtmp[:, 3, :], rtmp[:, 2, :], ALU.add), reads=[b_rtmp], writes=[b_rtmp])
                  P.op('dve', TS(rtmp[:, 2, :], rtmp[:, 3, :], float(-np.pi), ALU.is_lt, float(TWO_PI), ALU.mult),
                       reads=[b_rtmp], writes=[b_rtmp])
                  P.op('dve', TT(rtmp[:, 3, :], rtmp[:, 3, :], rtmp[:, 2, :], ALU.add), reads=[b_rtmp], writes=[b_rtmp])
                  P.op('act', ACTF(rtmp[:, 5, :], rtmp[:, 3, :], AF.Sin), reads=[b_rtmp], writes=[b_rtmp])
                  P.op('dve', CP(dst[:, :, :], bc(rtmp[:, 5:6, :], [128, 8, 32])), reads=[b_rtmp], writes=[b_dst])

              for l in range(L):
                  cur = l * 2 + (t % 2)
                  prv = l * 2 + ((t + 1) % 2)
                  P.dma('sp', DMA(ptm[:, :], ptm_d[l, :, :]), b_ptm, writes=[b_ptm])
                  P.op('act', ACTF(esink[:, :], ptm[:, 3072:3088], AF.Exp), reads=[b_ptm], writes=[b_esink])
                  chk(1)
                  pre_norm_T(l, PF_GMIX)
                  chk(2)
                  for cb in range(3):
                      proj_tok(t, l, f"qkv{cb}", hT, b_hT, 1 + cb)
                  P.op('act', ACTF(vr[cur][:, :, 0:64], bank(3)[:, 256:512].rearrange("p (h d) -> p h d", h=4), AF.Copy),
                       reads=[b_bank[3]], writes=[b_vr[cur]])
                  for (bk, nh, h0, c0_) in ((1, 8, 0, 0), (2, 8, 8, 0), (3, 4, 16, 0)):
                      src = bank(bk)[:, 0:nh * 64].rearrange("p (h d) -> p h d", h=nh)
                      x1 = src[:, :, 0:32]
                      x2 = src[:, :, 32:64]
                      cs_ = cos8[:, 0:nh, :]
                      sn_ = sin8[:, 0:nh, :]
                      s4 = scr[:, :].rearrange("p (k h d) -> p k h d", k=4, h=8)
                      P.op('dve', TT(s4[:, 0, 0:nh, :], x1, cs_, ALU.mult), reads=[b_bank[bk], b_cos8], writes=[b_scr])
                      P.op('dve', TT(s4[:, 1, 0:nh, :], x2, sn_, ALU.mult), reads=[b_bank[bk], b_sin8], writes=[b_scr])
                      P.op('dve', TT(s4[:, 2, 0:nh, :], x2, cs_, ALU.mult), reads=[b_bank[bk], b_cos8], writes=[b_scr])
                      P.op('dve', TT(s4[:, 3, 0:nh, :], x1, sn_, ALU.mult), reads=[b_bank[bk], b_sin8], writes=[b_scr])
                      P.op('pool', TT(qr3[:, h0:h0 + nh, 0:32], s4[:, 0, 0:nh, :], s4[:, 1, 0:nh, :], ALU.subtract),
                           reads=[b_scr], writes=[b_qr])
                      P.op('pool', TT(qr3[:, h0:h0 + nh, 32:64], s4[:, 2, 0:nh, :], s4[:, 3, 0:nh, :], ALU.add),
                           reads=[b_scr], writes=[b_qr])
                  for (bk, h0, nh) in ((4, 0, 8), (5, 8, 8), (6, 16, 4)):
                      P.mm([TR(bankb(bk)[0:64, j * 128:(j + 1) * 128], qr3[:, h0 + j, :], identb) for j in range(nh)],
                           reads=[b_qr, b_cstb], writes=[b_bank[bk]])
                  P.op('act', ACTF(qT3[0:64, 0:8, :], bankb(4)[0:64, :].rearrange("p (h t) -> p h t", h=8), AF.Copy),
                       reads=[b_bank[4]], writes=[b_qT])
                  P.op('dve', CP(qT3[0:64, 8:16, :], bankb(5)[0:64, :].rearrange("p (h t) -> p h t", h=8)),
                       reads=[b_bank[5]], writes=[b_qT])
                  P.op('act', ACTF(kTr[cur][:, :, :], bankb(6)[0:64, 0:512].rearrange("p (h t) -> p h t", h=4), AF.Copy),
                       reads=[b_bank[6]], writes=[b_kTr[cur]])
                  chk(3)
                  for g in range(4):
                      bs_c, bs_p, bo = (0, 1, 2) if g % 2 == 0 else (3, 4, 5)
                      rhs_q = qT[0:64, g * 512:(g + 1) * 512]
                      P.mm([MM(bank(bs_c), kTr[cur][:, g, :], rhs_q)], reads=[b_kTr[cur], b_qT], writes=[b_bank[bs_c]])
                      P.op('act', ACTF(pex[0][:, :], bank(bs_c), AF.Exp, scale=0.125), reads=[b_bank[bs_c]], writes=[b_pex[0]])
                      P.op('pool', TT(pm[0].rearrange("p (h t) -> p h t", h=4), pex[0].rearrange("p (h t) -> p h t", h=4),
                                      bc(mask_u.unsqueeze(1), [128, 4, 128]), ALU.mult),
                           reads=[b_pex[0], b_cstb], writes=[b_pm[0]])
                      if t > 0:
                          P.mm([MM(bank(bs_p), kTr[prv][:, g, :], rhs_q)], reads=[b_kTr[prv], b_qT], writes=[b_bank[bs_p]])
                          P.op('act', ACTF(pex[1][:, :], bank(bs_p), AF.Exp, scale=0.125), reads=[b_bank[bs_p]], writes=[b_pex[1]])
                          P.op('pool', TT(pm[1].rearrange("p (h t) -> p h t", h=4), pex[1].rearrange("p (h t) -> p h t", h=4),
                                          bc(mask_sl.unsqueeze(1), [128, 4, 128]), ALU.mult),
                               reads=[b_pex[1], b_cstb], writes=[b_pm[1]])
                      fns = []
                      for hq in range(4):
                          o_ = bank(bo)[:, hq * 65:(hq + 1) * 65]
                          if t > 0:
                              fns.append(MM(o_, pm[1][:, hq * 128:(hq + 1) * 128], vr[prv][:, g, :], start=True, stop=False))
                              fns.append(MM(o_, pm[0][:, hq * 128:(hq + 1) * 128], vr[cur][:, g, :], start=False, stop=True))
                          else:
                              fns.append(MM(o_, pm[0][:, hq * 128:(hq + 1) * 128], vr[cur][:, g, :], start=True, stop=True))
                      rd = [b_pm[0], b_vr[cur]] + ([b_pm[1], b_vr[prv]] if t > 0 else [])
                      P.mm(fns, reads=rd, writes=[b_bank[bo]])
                      o3 = bank(bo)[:, 0:260].rearrange("p (h d) -> p h d", h=4)
                      P.op('dve', TT(den[:, 0:4].unsqueeze(2), o3[:, :, 64:65], esink[:, g * 4:(g + 1) * 4].unsqueeze(2), ALU.add),
                           reads=[b_bank[bo], b_esink], writes=[b_den])
                      P.op('dve', RECIP(den[:, 4:8], den[:, 0:4]), reads=[b_den], writes=[b_den])
                      P.op('dve', TT(atok3[:, g * 4:(g + 1) * 4, :], o3[:, :, 0:64], bc(den[:, 4:8].unsqueeze(2), [128, 4, 64]), ALU.mult),
                           reads=[b_bank[bo], b_den], writes=[b_atok])
                  P.mm([TR(bankb(6)[:, j * 128:(j + 1) * 128], atok[:, j * 128:(j + 1) * 128], identb) for j in range(8)],
                       reads=[b_atok, b_cstb], writes=[b_bank[6]])
                  P.op('act', ACTF(AT[:, :, :], bankb(6)[:, :].rearrange("p (a t) -> p a t", a=8), AF.Copy),
                       reads=[b_bank[6]], writes=[b_AT])

                  chk(4)
                  for rb in range(6):
                      blk, b_blk = wget(t, l, f"rw{rb}")
                      bv = blk[:, :].rearrange("p (a b) -> p a b", a=8)
                      bk = rb % 2
                      fns = []
                      for fc in range(4):
                          for j in range(8):
                              fns.append(MM(bank(bk)[:, fc * 128:(fc + 1) * 128], bv[:, j, fc * 128:(fc + 1) * 128], hT[:, j, :],
                                            start=(j == 0), stop=(j == 7)))
                      P.mm(fns, reads=[b_hT, b_blk], writes=[b_bank[bk]])
                      P.op('act', ACTF(zb3[:, rb * 4:(rb + 1) * 4, 1:129], bank(bk).rearrange("p (c t) -> p c t", c=4), AF.Copy),
                           reads=[b_bank[bk]], writes=[b_zbuf])
                  blk, b_blk = wget(t, l, "lora1")
                  bv = blk[:, 0:2304].rearrange("p (a b) -> p a b", a=8)
                  fns = []
                  for fc, (c0_, cn) in enumerate(((0, 128), (128, 128), (256, 32))):
                      for j in range(8):
                          fns.append(MM(bank(2)[0:cn, fc * 128:(fc + 1) * 128], bv[:, j, c0_:c0_ + cn], hT[:, j, :],
                                        start=(j == 0), stop=(j == 7)))
                  P.mm(fns, reads=[b_hT, b_blk], writes=[b_bank[2]])
                  P.op('act', ACTF(zb3[:, 24:26, 1:129], bank(2)[:, 0:256].rearrange("p (c t) -> p c t", c=2), AF.Copy),
                       reads=[b_bank[2]], writes=[b_zbuf])
                  P.op('pool', MSET(zb3[:, 26, :], 0.0), writes=[b_zbuf])
                  P.op('act', ACTF(zb3[0:32, 26, 1:129], bank(2)[0:32, 256:384], AF.Copy), reads=[b_bank[2]], writes=[b_zbuf])
                  chk(5)
                  P.op('dve', CP(zb3[:, :, 0:1], zcar[l][:, :].unsqueeze(2)), reads=[b_zcar[l]], writes=[b_zbuf])
                  P.op('dve', TT(zs3[:, :, :], zb3[:, :, 0:128], zb3[:, :, 1:129], ALU.subtract), reads=[b_zbuf], writes=[b_zs])
                  P.op('pool', TT(zs3[:, :, :], zs3[:, :, :], bc(pfm[:, l, PF_MU:PF_MU + 27].unsqueeze(2), [128, 27, 128]), ALU.mult),
                       reads=[b_zs, b_pfm], writes=[b_zs])
                  P.op('dve', TT(zs3[:, :, :], zs3[:, :, :], zb3[:, :, 1:129], ALU.add), reads=[b_zs, b_zbuf], writes=[b_zs])
                  P.op('pool', CP(zcar[l][:, :].unsqueeze(2), zb3[:, :, 128:129]), reads=[b_zbuf], writes=[b_zcar[l]])
                  r3 = zs3[:, 0:8, :]
                  k3 = zs3[:, 8:16, :]
                  v3 = zs3[:, 16:24, :]
                  chk(6)
                  P.op('pool', MSET(la[:, 0:256], 0.0), writes=[b_la])
                  P.op('act', ACTF(la3[0:64, 0, :], zs3[0:64, 24, :], AF.Tanh), reads=[b_zs], writes=[b_la])
                  P.op('act', ACTF(la3[64:128, 1, :], zs3[64:128, 24, :], AF.Copy), reads=[b_zs], writes=[b_la])
                  P.op('act', ACTF(la3[:, 2, :], zs3[:, 25, :], AF.Sigmoid), reads=[b_zs], writes=[b_la])
                  P.op('act', ACTF(la3[:, 3, :], zs3[:, 26, :], AF.Sigmoid), reads=[b_zs], writes=[b_la])
                  chk(61)
                  blk, b_blk = wget(t, l, "lora2")
                  l2v = blk[:, 0:3072].rearrange("p (a b) -> p a b", a=3)
                  fw, fa, fg = [], [], []
                  for hp in range(8):
                      bko = hp // 4
                      cs_ = slice((hp % 4) * 128, (hp % 4 + 1) * 128)
                      fw.append(MM(bank(0 + bko)[:, cs_], l2v[:, 0, hp * 128:(hp + 1) * 128], la3[:, 0, :]))
                      fa.append(MM(bank(2 + bko)[:, cs_], l2v[:, 0, hp * 128:(hp + 1) * 128], la3[:, 1, :]))
                      fg.append(MM(bank(4 + bko)[:, cs_], l2v[:, 1, hp * 128:(hp + 1) * 128], la3[:, 2, :], start=True, stop=False))
                      fg.append(MM(bank(4 + bko)[:, cs_], l2v[:, 2, hp * 128:(hp + 1) * 128], la3[:, 3, :], start=False, stop=True))
                  P.mm(fw, reads=[b_la, b_blk], writes=[b_bank[0], b_bank[1]])
                  P.mm(fa, reads=[b_la, b_blk], writes=[b_bank[2], b_bank[3]])
                  P.mm(fg, reads=[b_la, b_blk], writes=[b_bank[4], b_bank[5]])

                  def psum2(i):
                      return pst[i][:, :].rearrange("p (a t) -> p a t", a=8)
                  chk(62)
                  P.op('dve', TT(F3[0], psum2(0), bc(pcol(l, PF_W0).unsqueeze(2), [128, 8, 128]), ALU.add),
                       reads=[b_bank[0], b_bank[1], b_pfm], writes=[b_F[0]])
                  P.op('act', ACTF(Fm[0], Fm[0], AF.Sigmoid), reads=[b_F[0]], writes=[b_F[0]])
                  P.op('dve', TT(F3[1], psum2(1), bc(pcol(l, PF_A0).unsqueeze(2), [128, 8, 128]), ALU.add),
                       reads=[b_bank[2], b_bank[3], b_pfm], writes=[b_F[1]])
                  P.op('act', ACTF(Fm[1], Fm[1], AF.Sigmoid), reads=[b_F[1]], writes=[b_F[1]])
                  chk(63)
                  P.op('dve', lambda e: e.tensor_tensor_scan(out=Fm[2], data0=scanm[:, :], data1=Fm[0], initial=0.0,
                                                             op0=ALU.mult, op1=ALU.add),
                       reads=[b_scanm, b_F[0]], writes=[b_F[2]])
                  chk(64)
                  ec3 = ecol.rearrange("p (k a) -> p k a", k=3)
                  P.op('act', ACTF(ec3[:, 0, :].unsqueeze(2), F3[2][:, :, 63:64], AF.Exp, scale=-C0), reads=[b_F[2]], writes=[b_ecol])
                  P.op('act', ACTF(ec3[:, 1, :].unsqueeze(2), F3[2][:, :, 127:128], AF.Exp, scale=-C0), reads=[b_F[2]], writes=[b_ecol])
                  P.op('pool', MSET(Hmid[:, :], 0.0), writes=[b_Hmid])
                  for half in range(2):
                      ps_ = slice(half * 64, (half + 1) * 64)
                      P.op('dve', TT(Hmbd4[ps_, :, half, :], Hst[l][ps_, :, :], bc(ec3[ps_, 0, :].unsqueeze(2), [64, 8, 64]), ALU.mult),
                           reads=[b_H[l], b_ecol], writes=[b_Hmid])
                  P.op('pool', TT(Hdec3, Hst[l][:, :, :], bc(ec3[:, 1, :].unsqueeze(2), [128, 8, 64]), ALU.mult),
                       reads=[b_H[l], b_ecol], writes=[b_Hdec])
                  P.op('dve', CP(gst[:, 0:8].unsqueeze(2), F3[2][:, :, 63:64]), reads=[b_F[2]], writes=[b_gst])
                  P.op('dve', TT(F3[2], F3[2], bc(gst[:, 0:8].unsqueeze(2), [128, 8, 128]), ALU.subtract),
                       reads=[b_F[2], b_gst], writes=[b_F[2]])
                  P.op('pool', TT(Fm[0], Fm[2], Fm[0], ALU.subtract), reads=[b_F[2], b_F[0]], writes=[b_F[0]])
                  P.op('act', ACTF(ec3[:, 2, :].unsqueeze(2), F3[2][:, :, 127:128], AF.Exp, scale=-C0), reads=[b_F[2]], writes=[b_ecol])
                  P.op('act', ACTF(Fm[0], Fm[0], AF.Exp, scale=-C0), reads=[b_F[0]], writes=[b_F[0]])
                  P.op('act', ACTF(Fm[3], Fm[2], AF.Exp, scale=-C0), reads=[b_F[2]], writes=[b_F[3]])
                  P.op('act', ACTF(Fm[2], Fm[2], AF.Exp, scale=C0), reads=[b_F[2]], writes=[b_F[2]])
                  P.op('dve', TT(AR4[:, :, 1, :], r3, F3[3], ALU.mult), reads=[b_zs, b_F[3]], writes=[b_AR])
                  chk(65)
                  P.op('dve', TT(F3[4], k3, bc(pcol(l, PF_KK).unsqueeze(2), [128, 8, 128]), ALU.mult),
                       reads=[b_zs, b_pfm], writes=[b_F[4]])
                  P.op('act', ACTF(Fm[5], Fm[4], AF.Square), reads=[b_F[4]], writes=[b_F[5]])
                  P.mm([MM(bank(6 + hp // 4)[:, (hp % 4) * 128:(hp % 4 + 1) * 128], onesf, F3[5][:, hp, :]) for hp in range(8)],
                       reads=[b_F[5], b_cstf], writes=[b_bank[6], b_bank[7]])
                  P.op('act', ACTF(Fm[5], pst[3][:, :], AF.Sqrt), reads=[b_bank[6], b_bank[7]], writes=[b_F[5]])
                  P.op('dve', TS(Fm[5], Fm[5], 1e-12, ALU.max), reads=[b_F[5]], writes=[b_F[5]])
                  P.op('dve', RECIP(Fm[5], Fm[5]), reads=[b_F[5]], writes=[b_F[5]])
                  P.op('pool', TT(Fm[4], Fm[4], Fm[5], ALU.mult), reads=[b_F[4], b_F[5]], writes=[b_F[4]])
                  chk(66)
                  P.op('dve', STT(AR4[:, :, 0, :], F3[4], -1.0, F3[0], ALU.mult, ALU.mult), reads=[b_F[4], b_F[0]], writes=[b_AR])
                  P.op('pool', TT(Fm[5], Fm[4], Fm[1], ALU.mult), reads=[b_F[4], b_F[1]], writes=[b_F[5]])
                  P.op('dve', TT(Btt, Fm[5], Fm[2], ALU.mult), reads=[b_F[5], b_F[2]], writes=[b_Bt])
                  P.op('dve', TS(Fm[5], Fm[1], -1.0, ALU.add), reads=[b_F[1]], writes=[b_F[5]])
                  P.op('pool', TT(F3[5], F3[5], bc(pcol(l, PF_KA).unsqueeze(2), [128, 8, 128]), ALU.mult),
                       reads=[b_F[5], b_pfm], writes=[b_F[5]])
                  P.op('dve', STT(F3[4], F3[5], 1.0, k3, ALU.add, ALU.mult), reads=[b_F[5], b_zs], writes=[b_F[4]])
                  P.op('pool', TT(Ktt, Fm[4], Fm[2], ALU.mult), reads=[b_F[4], b_F[2]], writes=[b_Kt])
                  P.op('act', ACTF(Vt3, v3, AF.Copy), reads=[b_zs], writes=[b_Vt])
                  P.op('dve', TT(F3[5], r3, F3[4], ALU.mult), reads=[b_zs, b_F[4]], writes=[b_F[5]])
                  P.op('pool', TT(F3[5], F3[5], bc(pcol(l, PF_RK).unsqueeze(2), [128, 8, 128]), ALU.mult),
                       reads=[b_F[5], b_pfm], writes=[b_F[5]])
                  P.mm([MM(bank(6 + hp // 4)[:, (hp % 4) * 128:(hp % 4 + 1) * 128], onesf, F3[5][:, hp, :]) for hp in range(8)],
                       reads=[b_F[5], b_cstf], writes=[b_bank[6], b_bank[7]])
                  P.op('dve', TT(F3[5], psum2(3), v3, ALU.mult), reads=[b_bank[6], b_bank[7], b_zs], writes=[b_F[5]])
                  P.op('act', ACTF(F3[1], psum2(2), AF.Copy), reads=[b_bank[4], b_bank[5]], writes=[b_F[1]])
                  chk(67)
                  for (src3, b_src, bk, dstt, b_dstt, eng) in ((Kt3, b_Kt, 0, Ktok, b_Ktok, 'act'), (Bt3, b_Bt, 1, Btok, b_Btok, 'dve'),
                                                               (Vt3, b_Vt, 2, Vtok, b_Vtok, 'act')):
                      P.mm([TR(bankb(bk)[:, hp * 128:(hp + 1) * 128], src3[:, hp, :], identb) for hp in range(8)],
                           reads=[b_src, b_cstb], writes=[b_bank[bk]])
                      if eng == 'act':
                          P.op('act', ACTF(dstt, bankb(bk)[:, :], AF.Copy), reads=[b_bank[bk]], writes=[b_dstt])
                      else:
                          P.op('dve', CP(dstt, bankb(bk)[:, :]), reads=[b_bank[bk]], writes=[b_dstt])
                  chk(7)
                  P.op('pool', MSET(ARbd[:, :], 0.0), writes=[b_ARbd])
                  P.op('dve', CP(ARbd5[0:64, :, 0, :, :], AR4[0:64, :, :, :]), reads=[b_AR], writes=[b_ARbd])
                  P.op('pool', CP(ARbd5[64:128, :, 1, :, :], AR4[64:128, :, :, :]), reads=[b_AR], writes=[b_ARbd])
                  for g in range(4):
                      f1, f2 = [], []
                      for i in range(2):
                          hp = 2 * g + i
                          f1.append(MM(bank(0 + i), Bt3[:, hp, :], ARbd[:, hp * 512:(hp + 1) * 512]))
                          f2.append(MM(bank(2 + i), Kt3[:, hp, :], ARbd[:, hp * 512:(hp + 1) * 512]))
                      P.mm(f1, reads=[b_Bt, b_ARbd], writes=[b_bank[0], b_bank[1]])
                      P.mm(f2, reads=[b_Kt, b_ARbd], writes=[b_bank[2], b_bank[3]])
                      pa4 = pst[0][:, :].rearrange("p (h b t) -> p h b t", h=4, b=2)
                      pb4 = pst[1][:, :].rearrange("p (h b t) -> p h b t", h=4, b=2)
                      P.op('dve', TT(Mf3[0], pa4[:, :, 0, :], bc(mask_su.unsqueeze(1), [128, 4, 128]), ALU.mult),
                           reads=[b_bank[0], b_bank[1], b_cstb], writes=[b_Mf[0]])
                      P.op('dve', TT(MA4[:, :, 1, :], pa4[:, :, 1, :], bc(mask_u.unsqueeze(1), [128, 4, 128]), ALU.mult),
                           reads=[b_bank[0], b_bank[1], b_cstb], writes=[b_MA])
                      P.op('dve', TT(KA4[:, :, 0, :], pb4[:, :, 0, :], bc(mask_su.unsqueeze(1), [128, 4, 128]), ALU.mult),
                           reads=[b_bank[2], b_bank[3], b_cstb], writes=[b_KA])
                      P.op('dve', TT(KA4[:, :, 1, :], pb4[:, :, 1, :], bc(mask_u.unsqueeze(1), [128, 4, 128]), ALU.mult),
                           reads=[b_bank[2], b_bank[3], b_cstb], writes=[b_KA])
                      P.mm([TR(bank(4)[:, hh * 128:(hh + 1) * 128], Mf3[0][:, hh, :], identf) for hh in range(4)],
                           reads=[b_Mf[0], b_cstf], writes=[b_bank[4]])
                      P.op('act', ACTF(Lf[0], bank(4), AF.Copy), reads=[b_bank[4]], writes=[b_Lf[0]])
                      fns = []
                      for i in range(2):
                          hp = 2 * g + i
                          fns.append(MM(bank(5)[:, i * 128:(i + 1) * 128], AR4[:, hp, 0, :], Hmid[:, hp * 128:(hp + 1) * 128], start=True, stop=False))
                          for j in range(2):
                              hh = 2 * i + j
                              h = 4 * g + hh
                              fns.append(MM(bank(5)[:, hh * 64:(hh + 1) * 64], KA4[:, hh, 0, :], Vtok[:, h * 64:(h + 1) * 64], start=False, stop=(j == 1)))
                      P.mm(fns, reads=[b_AR, b_Hmid, b_KA, b_Vtok], writes=[b_bank[5]])
                      P.op('act', ACTF(Uf[0], bank(5)[:, 0:256], AF.Copy), reads=[b_bank[5]], writes=[b_Uf[0]])
                      for k in range(7):
                          ci, ni = k % 2, (k + 1) % 2
                          P.mm([MM(bank(5)[:, hh * 64:(hh + 1) * 64], Mf3[ci][:, hh, :], Uf3[ci][:, hh, :]) for hh in range(4)],
                               reads=[b_Mf[ci], b_Uf[ci]], writes=[b_bank[5]])
                          P.op('dve', TT(Uf[ni], bank(5)[:, 0:256], Uf[ci], ALU.add), reads=[b_bank[5], b_Uf[ci]], writes=[b_Uf[ni]])
                          if k < 6:
                              P.mm([MM(bank(6)[:, hh * 128:(hh + 1) * 128], Lf3[ci][:, hh, :], Mf3[ci][:, hh, :]) for hh in range(4)],
                                   reads=[b_Mf[ci], b_Lf[ci]], writes=[b_bank[6]])
                              P.op('act', ACTF(Mf[ni], bank(6), AF.Copy), reads=[b_bank[6]], writes=[b_Mf[ni]])
                          if k < 5:
                              P.mm([MM(bank(7)[:, hh * 128:(hh + 1) * 128], Mf3[ci][:, hh, :], Lf3[ci][:, hh, :]) for hh in range(4)],
                                   reads=[b_Mf[ci], b_Lf[ci]], writes=[b_bank[7]])
                              P.op('dve', CP(Lf[ni], bank(7)), reads=[b_bank[7]], writes=[b_Lf[ni]])
                      P.op('act', ACTF(Ub3[:, 4 * g:4 * g + 4, :], Uf3[1], AF.Copy), reads=[b_Uf[1]], writes=[b_Ub])
                      fns = []
                      for i in range(2):
                          hp = 2 * g + i
                          fns.append(MM(bank(4)[:, i * 128:(i + 1) * 128], AR4[:, hp, 1, :], Hmid[:, hp * 128:(hp + 1) * 128], start=True, stop=False))
                          for j in range(2):
                              hh = 2 * i + j
                              h = 4 * g + hh
                              o_ = bank(4)[:, hh * 64:(hh + 1) * 64]
                              fns.append(MM(o_, MA4[:, hh, 1, :], Ub3[:, h, :], start=False, stop=False))
                              fns.append(MM(o_, KA4[:, hh, 1, :], Vtok[:, h * 64:(h + 1) * 64], start=False, stop=(j == 1)))
                      P.mm(fns, reads=[b_AR, b_Hmid, b_MA, b_Ub, b_KA, b_Vtok], writes=[b_bank[4]])
                      P.op('act', ACTF(scr[:, g * 256:(g + 1) * 256], bank(4)[:, 0:256], AF.Copy), reads=[b_bank[4]], writes=[b_scr])
                  chk(8)
                  fns = []
                  for hp in range(8):
                      o_ = bank(hp // 4)[:, (hp % 4) * 128:(hp % 4 + 1) * 128]
                      fns.append(MM(o_, Btok[:, hp * 128:(hp + 1) * 128], Ub[:, hp * 128:(hp + 1) * 128], start=True, stop=False))
                      fns.append(MM(o_, Ktok[:, hp * 128:(hp + 1) * 128], Vtok[:, hp * 128:(hp + 1) * 128], start=False, stop=True))
                  P.mm(fns, reads=[b_Btok, b_Ub, b_Ktok, b_Vtok], writes=[b_bank[0], b_bank[1]])
                  dH = psum2(0)
                  for half in range(2):
                      ps_ = slice(half * 64, (half + 1) * 64)
                      P.op('dve', TT(Hst[l][ps_, :, :], dH[ps_, :, half * 64:(half + 1) * 64],
                                     bc(ec3[ps_, 2, :].unsqueeze(2), [64, 8, 64]), ALU.mult),
                           reads=[b_bank[0], b_bank[1], b_ecol], writes=[b_H[l]])
                  P.op('pool', TT(Hst[l][:, :, :], Hst[l][:, :, :], Hdec3, ALU.add), reads=[b_H[l], b_Hdec], writes=[b_H[l]])
                  y3 = scr[:, :].rearrange("p (h v) -> p h v", h=16)
                  P.op('dve', RSUM(gst[:, 0:16], y3), reads=[b_scr], writes=[b_gst])
                  P.op('dve', TS(gst[:, 0:16], gst[:, 0:16], 1.0 / 64, ALU.mult), reads=[b_gst], writes=[b_gst])
                  P.op('dve', TT(y3, y3, bc(gst[:, 0:16].unsqueeze(2), [128, 16, 64]), ALU.subtract), reads=[b_scr, b_gst], writes=[b_scr])
                  F43 = Fm[4].rearrange("p (h v) -> p h v", h=16)
                  P.op('act', ACTF(Fm[4], scr[:, :], AF.Square), reads=[b_scr], writes=[b_F[4]])
                  P.op('dve', RSUM(gst[:, 16:32], F43), reads=[b_F[4]], writes=[b_gst])
                  P.op('act', ACTF(gst[:, 16:32], gst[:, 16:32], AF.Sqrt, scale=1.0 / 64, bias=64e-5), reads=[b_gst], writes=[b_gst])
                  P.op('dve', RECIP(gst[:, 16:32], gst[:, 16:32]), reads=[b_gst], writes=[b_gst])
                  P.op('dve', TT(y3, y3, bc(gst[:, 16:32].unsqueeze(2), [128, 16, 64]), ALU.mult), reads=[b_scr, b_gst], writes=[b_scr])
                  P.mm([TR(bank(2 + hp // 4)[:, (hp % 4) * 128:(hp % 4 + 1) * 128], scr[:, hp * 128:(hp + 1) * 128], identf) for hp in range(8)],
                       reads=[b_scr, b_cstf], writes=[b_bank[2], b_bank[3]])
                  P.op('dve', TT(F3[4], psum2(1), bc(pcol(l, PF_GNW).unsqueeze(2), [128, 8, 128]), ALU.mult),
                       reads=[b_bank[2], b_bank[3], b_pfm], writes=[b_F[4]])
                  P.op('pool', TT(F3[4], F3[4], bc(pcol(l, PF_GNB).unsqueeze(2), [128, 8, 128]), ALU.add), reads=[b_F[4], b_pfm], writes=[b_F[4]])
                  P.op('dve', TT(Fm[4], Fm[4], Fm[5], ALU.add), reads=[b_F[4], b_F[5]], writes=[b_F[4]])
                  P.op('pool', TT(BT[:, :, :], F3[4], F3[1], ALU.mult), reads=[b_F[4], b_F[1]], writes=[b_BT])

                  chk(9)
                  for br, (XT_, b_XT) in enumerate(((AT, b_AT), (BT, b_BT))):
                      for cb in range(2):
                          proj_tok(t, l, f"g{br}{cb}", hT, b_hT, 0)
                          proj_tok(t, l, f"wo{br}{cb}", XT_, b_XT, 1)
                          P.op('act', ACTF(sig[cb], bank(0), AF.Sigmoid), reads=[b_bank[0]], writes=[b_sig[cb]])
                          dstm = mixed[:, cb * 512:(cb + 1) * 512]
                          if br == 0:
                              P.op('dve', TT(dstm, bank(1), sig[cb], ALU.mult), reads=[b_bank[1], b_sig[cb]], writes=[b_mixed])
                          else:
                              P.op('dve', TT(sig[cb], bank(1), sig[cb], ALU.mult), reads=[b_bank[1], b_sig[cb]], writes=[b_sig[cb]])
                              P.op('pool', TT(mixb[:, cb * 512:(cb + 1) * 512], dstm, sig[cb], ALU.add),
                                   reads=[b_mixed, b_sig[cb]], writes=[b_mixb])
                  P.mm([TR(bankb(2)[:, j * 128:(j + 1) * 128], mixb[:, j * 128:(j + 1) * 128], identb) for j in range(8)],
                       reads=[b_mixb, b_cstb], writes=[b_bank[2]])
                  P.op('act', ACTF(mT3, bankb(2)[:, :].rearrange("p (a t) -> p a t", a=8), AF.Copy), reads=[b_bank[2]], writes=[b_mT])
                  for cb in range(2):
                      proj_tok(t, l, f"wout{cb}", mT3, b_mT, 3 + cb)
                  post_norm_add([bank(3), bank(4)], [b_bank[3], b_bank[4]], 0)

                  chk(10)
                  pre_norm_T(l, PF_GFFN)
                  for i in range(11):
                      blk, b_blk = wget(t, l, f"up{i}")
                      bv = blk[:, :].rearrange("p (a b) -> p a b", a=8)
                      bk = 5 + (i % 2)
                      fns = []
                      for c4 in range(4):
                          for j in range(8):
                              fns.append(MM(bank(bk)[:, c4 * 128:(c4 + 1) * 128], bv[:, j, c4 * 128:(c4 + 1) * 128], hT[:, j, :],
                                            start=(j == 0), stop=(j == 7)))
                      P.mm(fns, reads=[b_hT, b_blk], writes=[b_bank[bk]])
                      gcs = [2 * i, 22 + 2 * i, 2 * i + 1, 22 + 2 * i + 1]
                      psu = bank(bk).rearrange("p (c t) -> p c t", c=4)
                      P.op('act', ACTF(ub3[:, :, 2:130], psu, AF.Copy), reads=[b_bank[bk]], writes=[b_ubuf])
                      for c4, gc in enumerate(gcs):
                          P.op('pool', CP(ub3[:, c4, 0:2], ccar[l][:, gc, :]), reads=[b_ccar[l]], writes=[b_ubuf])
                      for c4, gc in enumerate(gcs):
                          P.op('pool', CP(ccar[l][:, gc, :], ub3[:, c4, 128:130]), reads=[b_ubuf], writes=[b_ccar[l]])
                      for c4, gc in enumerate(gcs):
                          w2c = pfm[:, l, PF_CW + 2 * 44 + gc: PF_CW + 2 * 44 + gc + 1]
                          w1c = pfm[:, l, PF_CW + 1 * 44 + gc: PF_CW + 1 * 44 + gc + 1]
                          w0c = pfm[:, l, PF_CW + 0 * 44 + gc: PF_CW + 0 * 44 + gc + 1]
                          bcv = pfm[:, l, PF_CB + gc: PF_CB + gc + 1]
                          P.op('dve', TS(cacc3[:, c4, :], ub3[:, c4, 2:130], w2c, ALU.mult, bcv, ALU.add),
                               reads=[b_ubuf, b_pfm], writes=[b_cacc])
                          P.op('dve', STT(cacc3[:, c4, :], ub3[:, c4, 1:129], w1c, cacc3[:, c4, :], ALU.mult, ALU.add),
                               reads=[b_ubuf, b_pfm, b_cacc], writes=[b_cacc])
                          P.op('dve', STT(cacc3[:, c4, :], ub3[:, c4, 0:128], w0c, cacc3[:, c4, :], ALU.mult, ALU.add),
                               reads=[b_ubuf, b_pfm, b_cacc], writes=[b_cacc])
                      ca = cacc.rearrange("p (i ab t) -> p i ab t", i=2, ab=2)
                      P.op('act', ACTF(gel3, ca[:, :, 0, :], AF.Gelu_apprx_tanh), reads=[b_cacc], writes=[b_gel])
                      P.op('pool', TT(gmT3[:, 2 * i:2 * i + 2, :], gel3, ca[:, :, 1, :], ALU.mult), reads=[b_gel, b_cacc], writes=[b_gmT])
                  for db in range(6):
                      nj = 4 if db < 5 else 2
                      blk, b_blk = wget(t, l, f"down{db}")
                      bv = blk[:, 0:nj * 1024].rearrange("p (a b) -> p a b", a=nj)
                      fns = []
                      for jj in range(nj):
                          fidx = 4 * db + jj
                          for cb in range(2):
                              fns.append(MM(bank(3 + cb), gmT3[:, fidx, :], bv[:, jj, cb * 512:(cb + 1) * 512],
                                            start=(fidx == 0), stop=(fidx == 21)))
                      P.mm(fns, reads=[b_gmT, b_blk], writes=[b_bank[3], b_bank[4]])
                  post_norm_add([bank(3), bank(4)], [b_bank[3], b_bank[4]], 1024)

                  chk(11)
                  blk, b_blk = wget(t, l, "ple")
                  bv = blk[:, 0:2048].rearrange("p (a b) -> p a b", a=2)
                  for cb in range(2):
                      P.mm([MM(bank(5 + cb), pTb[:, l, c, :], bv[:, c, cb * 512:(cb + 1) * 512], start=(c == 0), stop=(c == 1)) for c in range(2)],
                           reads=[b_pTb, b_blk], writes=[b_bank[5 + cb]])
                  P.op('act', ACTF(hn[:, :], xt[:, :], AF.Copy), reads=[b_x], writes=[b_hn])
                  P.mm([TR(bankb(0)[:, j * 128:(j + 1) * 128], hn[:, j * 128:(j + 1) * 128], identb) for j in range(8)],
                       reads=[b_hn, b_cstb], writes=[b_bank[0]])
                  P.op('dve', CP(hT[:, :, :], bankb(0)[:, :].rearrange("p (a b) -> p a b", a=8)), reads=[b_bank[0]], writes=[b_hT])
                  for cb in range(2):
                      proj_tok(t, l, f"pg{cb}", hT, b_hT, 1 + cb)
                      P.op('act', ACTF(sig[cb], bank(1 + cb), AF.Sigmoid), reads=[b_bank[1 + cb]], writes=[b_sig[cb]])
                      P.op('dve', TT(mixed[:, cb * 512:(cb + 1) * 512], bank(5 + cb), sig[cb], ALU.mult),
                           reads=[b_bank[5 + cb], b_sig[cb]], writes=[b_mixed])
                  post_norm_add([mixed[:, 0:512], mixed[:, 512:1024]], [b_mixed, b_mixed], 2048)
              P.dma('sp', DMA(out_d[tok0:tok0 + 128, :], xt[:, :]), b_x, reads=[b_x], writes=[Buf("o")])
        except _Stop:
            P.wait_all('sp', b_ring)
            P.dma('sp', DMA(out_d[0:128, :], xt[:, :]), b_x, reads=[b_x], writes=[Buf('o')])
        P.wait_all('sp', [b_x])
        P.emit()
    return nc


def _prep(inputs, S, L):
    blocks = [_layer_blocks(inputs, l) for l in range(L)]
    meta, off = [], 0
    for name, arr in blocks[0]:
        meta.append((name, off, arr.shape[1]))
        off += arr.shape[1]
    TOT = off
    wcat = np.stack([np.concatenate([a for _, a in blocks[l]], axis=1) for l in range(L)]).astype(np.float32)
    pfm = np.stack([_layer_pfm(inputs, l) for l in range(L)], axis=1).astype(np.float32)
    ptm = np.stack([_layer_ptm(inputs, l) for l in range(L)]).astype(np.float32)
    return meta, TOT, wcat, np.ascontiguousarray(pfm), ptm


def _run(inputs, core_ids, S, L, B):
    inputs = {k: np.asarray(v) for k, v in inputs.items()}
    meta, TOT, wcat, pfm, ptm = _prep(inputs, S, L)
    cst = _consts()
    import time as _t
    _t0 = _t.time()
    nc = build(S, L, meta, TOT)
    print('build_s', _t.time() - _t0, flush=True)
    in_maps = []
    for b in range(B):
        p = inputs['p'][:L, b]
        pT = np.ascontiguousarray(p.reshape(L, S, 2, 128).transpose(3, 0, 2, 1)).astype(np.float32)
        in_maps.append({
            "x": np.ascontiguousarray(inputs['x'][b]).astype(np.float32),
            "pT": pT,
            "pos": np.ascontiguousarray(inputs['positions'][b].reshape(S, 1)).astype(np.int32),
            "wcat": wcat, "pfm": pfm, "ptm": ptm, "cst": cst,
        })
    res = run_bass_kernel_spmd(nc, in_maps, core_ids=core_ids)
    return np.stack([res.results[b]["out"] for b in range(B)]).astype(np.float32)


def kernel(**inputs):
    return _run(inputs, [0, 1], 8192, NL, 2)
```

```python
import numpy as np
from contextlib import ExitStack
import concourse.bass as bass
import concourse.mybir as mybir
from concourse.bass_utils import run_bass_kernel_spmd

F32 = mybir.dt.float32
BF16 = mybir.dt.bfloat16
F32R = mybir.dt.float32r
I32 = mybir.dt.int32
AF = mybir.ActivationFunctionType
ALU = mybir.AluOpType
AX = mybir.AxisListType

D = 1024
NL = 2
FFN = 2816
SEG = 8000
DMA_LIM = 48000
SAME_ENGINE_SYNC = True
NSLOT = 3
C0 = float(np.exp(-0.5))
TWO_PI = 2.0 * np.pi
STOP = None


class _Stop(Exception):
    pass


def chk(n):
    if STOP is not None and n == STOP:
        raise _Stop()


class Buf:
    __slots__ = ('w', 'r', 'name', 'dsem', 'dcnt', 'dlast', 'rng')

    def __init__(self, name='', rng=None):
        self.w = None
        self.r = {}
        self.name = name
        self.dsem = None
        self.dcnt = 0
        self.dlast = None
        self.rng = rng


class Prog:
    ENGS = ('pe', 'act', 'dve', 'pool', 'sp')

    def __init__(self, nc, stack):
        self.nc = nc
        self.stack = stack
        self.streams = {e: [] for e in self.ENGS}
        self.cnt = {e: 0 for e in self.ENGS}
        self.sems = {e: [] for e in self.ENGS}
        self.seen = {e: {} for e in self.ENGS}
        self.nsem = 0
        self.arena_bufs = []

    def new_sem(self, name):
        self.nsem += 1
        return self.stack.enter_context(self.nc.semaphore(f"s{self.nsem}_{name}"))

    def _tok(self, eng):
        k = self.cnt[eng]
        self.cnt[eng] = k + 1
        seg = k // SEG
        while len(self.sems[eng]) <= seg:
            self.sems[eng].append(self.new_sem(eng))
        return (self.sems[eng][seg], (k % SEG) + 1, eng)

    def _expand(self, bufs):
        out = []
        for b in bufs:
            out.append(b)
            if b.rng is not None:
                s, e = b.rng
                for o in self.arena_bufs:
                    if o is not b and o.rng[0] < e and s < o.rng[1]:
                        out.append(o)
        return out

    def _collect(self, eng, reads, writes):
        waits = {}

        def add(t):
            if t is None:
                return
            s, v, src = t
            if src == eng and (eng == 'pe' or not SAME_ENGINE_SYNC):
                return
            key = id(s)
            if self.seen[eng].get(key, 0) >= v:
                return
            if key not in waits or waits[key][1] < v:
                waits[key] = (s, v)
        for b in self._expand(reads):
            add(b.w)
        for b in self._expand(writes):
            add(b.w)
            for t in b.r.values():
                add(t)
        for key, (s, v) in waits.items():
            self.seen[eng][key] = v
        return list(waits.values())

    def _commit(self, tok, reads, writes):
        key = id(tok[0])
        for b in reads:
            b.r[key] = tok
        for b in writes:
            b.w = tok
            b.r = {}

    def op(self, eng, fn, reads=(), writes=()):
        waits = self._collect(eng, reads, writes)
        tok = self._tok(eng)
        self.streams[eng].append((waits, fn, tok[0], 1))
        self._commit(tok, reads, writes)
        return tok

    def mm(self, fns, reads=(), writes=()):
        waits = self._collect('pe', reads, writes)
        for f in fns[:-1]:
            self.streams['pe'].append((waits, f, None, 0))
            waits = []
        tok = self._tok('pe')
        self.streams['pe'].append((waits, fns[-1], tok[0], 1))
        self._commit(tok, reads, writes)
        return tok

    def dma(self, eng, fn, owner, reads=(), writes=()):
        if owner.dsem is None or owner.dcnt + 16 > DMA_LIM:
            owner.dsem = self.new_sem('d' + owner.name)
            owner.dcnt = 0
            owner.dlast = None
        waits = self._collect(eng, reads, writes)
        if owner.dlast is not None:
            s, v, _ = owner.dlast
            if self.seen[eng].get(id(s), 0) < v:
                self.seen[eng][id(s)] = v
                waits.append((s, v))
        owner.dcnt += 16
        tok = (owner.dsem, owner.dcnt, 'dma')
        owner.dlast = tok
        self.streams[eng].append((waits, fn, tok[0], 16))
        self._commit(tok, reads, writes)
        return tok

    def wait_all(self, eng, bufs):
        waits = self._collect(eng, bufs, bufs)
        self.streams[eng].append((waits, None, None, 0))

    def emit(self):
        nc = self.nc
        engs = {'pe': 'tensor', 'act': 'scalar', 'dve': 'vector', 'pool': 'gpsimd', 'sp': 'sync'}
        with nc.Block() as block:
            for k, attr in engs.items():
                stream = self.streams[k]

                def body(e, stream=stream):
                    for waits, fn, sem, inc in stream:
                        for s, v in waits:
                            e.wait_ge(s, v)
                        if fn is None:
                            continue
                        inst = fn(e)
                        if sem is not None:
                            inst.then_inc(sem, inc)
                getattr(block, attr)(body)


def MM(out, lhsT, rhs, start=True, stop=True):
    return lambda e: e.matmul(out, lhsT=lhsT, rhs=rhs, start=start, stop=stop)


def MMR(out, lhsT, rhs, start=True, stop=True):
    return lambda e: e.matmul(out, lhsT=lhsT.bitcast(F32R), rhs=rhs.bitcast(F32R), start=start, stop=stop)


def TR(out, in_, ident):
    return lambda e: e.transpose(out=out, in_=in_, identity=ident)


def ACTF(out, in_, func, scale=None, bias=None):
    kw = {}
    if scale is not None:
        kw['scale'] = scale
    if bias is not None:
        kw['bias'] = bias
    return lambda e: e.activation(out=out, in_=in_, func=func, **kw)


def TT(out, in0, in1, op):
    return lambda e: e.tensor_tensor(out=out, in0=in0, in1=in1, op=op)


def TS(out, in0, s1, op0, s2=None, op1=None):
    if op1 is None:
        return lambda e: e.tensor_scalar(out=out, in0=in0, scalar1=s1, scalar2=None, op0=op0)
    return lambda e: e.tensor_scalar(out=out, in0=in0, scalar1=s1, scalar2=s2, op0=op0, op1=op1)


def STT(out, in0, scalar, in1, op0, op1):
    return lambda e: e.scalar_tensor_tensor(out=out, in0=in0, scalar=scalar, in1=in1, op0=op0, op1=op1)


def CP(out, in_):
    return lambda e: e.tensor_copy(out=out, in_=in_)


def RSUM(out, in_):
    return lambda e: e.tensor_reduce(out=out, in_=in_, axis=AX.X, op=ALU.add)


def RECIP(out, in_):
    return lambda e: e.reciprocal(out=out, in_=in_)


def MSET(ap, v):
    return lambda e: e.memset(ap, v)


def DMA(out, in_):
    return lambda e: e.dma_start(out=out, in_=in_)


def _blk_proj(W, cols):
    sub = np.ascontiguousarray(W[:, cols])
    return sub.reshape(8, 128, -1).transpose(1, 0, 2).reshape(128, -1)


def _layer_blocks(inp, l):
    W_in = inp['w_in'][l]
    blocks = []
    for cb in range(3):
        blocks.append((f"qkv{cb}", _blk_proj(W_in, np.arange(cb * 512, (cb + 1) * 512))))
    for rb in range(6):
        blocks.append((f"rw{rb}", _blk_proj(W_in, 1536 + np.arange(rb * 512, (rb + 1) * 512))))
    blocks.append(("lora1", _blk_proj(W_in, np.arange(4608, 4896))))
    l2 = np.zeros((128, 3, 1024), np.float32)
    l2[0:64, 0] = inp['rw_w2'][l]
    l2[64:128, 0] = inp['rw_a2'][l]
    l2[:, 1] = inp['rw_g2'][l][0:128]
    l2[0:32, 2] = inp['rw_g2'][l][128:160]
    blocks.append(("lora2", l2.reshape(128, -1)))
    for br, (goff, wo) in enumerate(((4896, inp['w_o_att'][l]), (5920, inp['w_o_rw'][l]))):
        for cb in range(2):
            blocks.append((f"g{br}{cb}", _blk_proj(W_in, goff + np.arange(cb * 512, (cb + 1) * 512))))
            blocks.append((f"wo{br}{cb}", _blk_proj(wo, np.arange(cb * 512, (cb + 1) * 512))))
    for cb in range(2):
        blocks.append((f"wout{cb}", _blk_proj(inp['w_out'][l], np.arange(cb * 512, (cb + 1) * 512))))
    W_up = inp['w_up'][l]
    for i in range(11):
        cols = []
        for c in (2 * i, 2 * i + 1):
            cols.append(np.arange(c * 128, (c + 1) * 128))
            cols.append(FFN + np.arange(c * 128, (c + 1) * 128))
        blocks.append((f"up{i}", _blk_proj(W_up, np.concatenate(cols))))
    Wd = inp['w_down'][l]
    for db in range(6):
        nj = 4 if db < 5 else 2
        a = Wd[db * 512: db * 512 + nj * 128].reshape(nj, 128, 1024).transpose(1, 0, 2).reshape(128, -1)
        blocks.append((f"down{db}", a))
    blocks.append(("ple", inp['w_ple'][l].reshape(2, 128, 1024).transpose(1, 0, 2).reshape(128, -1)))
    for cb in range(2):
        blocks.append((f"pg{cb}", _blk_proj(inp['w_ple_gate'][l], np.arange(cb * 512, (cb + 1) * 512))))
    return blocks


NPF = 27 + 9 * 8 + 132 + 44
PF_MU, PF_W0, PF_A0, PF_KK, PF_KA, PF_GNW, PF_GNB, PF_RK, PF_GMIX, PF_GFFN, PF_CW, PF_CB = (
    0, 27, 35, 43, 51, 59, 67, 75, 83, 91, 99, 231)
NPT = 3 * 1024 + 16


def _layer_pfm(inp, l):
    a = np.zeros((128, NPF), np.float32)
    mu = inp['rw_mu'][l]
    a[:, 0:26] = mu[0:26 * 128].reshape(26, 128).T
    a[0:32, 26] = mu[26 * 128:]
    for off, key in ((PF_W0, 'rw_w0'), (PF_A0, 'rw_a0'), (PF_KK, 'rw_k_k'), (PF_KA, 'rw_k_a'),
                     (PF_GNW, 'rw_gn_w'), (PF_GNB, 'rw_gn_b'), (PF_RK, 'rw_r_k'),
                     (PF_GMIX, 'norm_mix_pre'), (PF_GFFN, 'norm_ffn_pre')):
        a[:, off:off + 8] = inp[key][l].reshape(8, 128).T
    cw = inp['conv_w'][l]
    a[:, PF_CW:PF_CW + 132] = cw.reshape(3, 44, 128).transpose(2, 0, 1).reshape(128, 132)
    a[:, PF_CB:PF_CB + 44] = inp['conv_b'][l].reshape(44, 128).T
    return a


def _layer_ptm(inp, l):
    row = np.concatenate([inp['norm_mix_post'][l], inp['norm_ffn_post'][l], inp['norm_ple'][l],
                          inp['att_sinks'][l]]).astype(np.float32)
    return np.ascontiguousarray(np.broadcast_to(row[None, :], (128, NPT)))


CS_ID, CS_MU, CS_MSU, CS_MSL, CS_ONES, CS_IF, CS_SCAN = 0, 128, 256, 384, 512, 640, 672
NCST = 672 + 1024


def _consts():
    c = np.zeros((128, NCST), np.float32)
    r = np.arange(128)[:, None]
    q = np.arange(128)[None, :]
    c[:, CS_ID:CS_ID + 128] = np.eye(128)
    c[:, CS_MU:CS_MU + 128] = (r <= q)
    c[:, CS_MSU:CS_MSU + 128] = (r < q)
    c[:, CS_MSL:CS_MSL + 128] = (r > q)
    c[:, CS_ONES:CS_ONES + 128] = ((r // 64) == (q // 64))
    half = 32
    c[:, CS_IF:CS_IF + 32] = np.power(np.float32(10000.0), -np.arange(half, dtype=np.float32) / half)[None, :]
    sm = np.ones(1024, np.float32)
    sm[::128] = 0.0
    c[:, CS_SCAN:CS_SCAN + 1024] = sm[None, :]
    return c


def build(S, L, blocks_meta, TOT):
    NT = S // 128
    NB = len(blocks_meta)
    nc = bass.Bass("TRN2", target_bir_lowering=False)
    x_d = nc.dram_tensor("x", [S, D], F32, kind="ExternalInput").ap()
    pT_d = nc.dram_tensor("pT", [128, L, 2, S], F32, kind="ExternalInput").ap()
    pos_d = nc.dram_tensor("pos", [S, 1], I32, kind="ExternalInput").ap()
    w_d = nc.dram_tensor("wcat", [L, 128, TOT], F32, kind="ExternalInput").ap()
    pfm_d = nc.dram_tensor("pfm", [128, L, NPF], F32, kind="ExternalInput").ap()
    ptm_d = nc.dram_tensor("ptm", [L, 128, NPT], F32, kind="ExternalInput").ap()
    cst_d = nc.dram_tensor("cst", [128, NCST], F32, kind="ExternalInput").ap()
    wb_d = nc.dram_tensor("wbf", [L, 128, TOT], BF16, kind="Internal").ap()
    out_d = nc.dram_tensor("out", [S, D], F32, kind="ExternalOutput").ap()

    with ExitStack() as st:
        P = Prog(nc, st)

        def sb(name, shape, dt):
            return st.enter_context(nc.sbuf_tensor("s_" + name, shape, dt)), Buf(name)

        ring, b_ring = [], []
        for i in range(NSLOT):
            t_, b_ = sb(f"ring{i}", [128, 4096], BF16)
            ring.append(t_)
            b_ring.append(b_)
        xt, b_x = sb("xt", [128, D], F32)
        cstf, b_cstf = sb("cstf", [128, NCST], F32)
        cstb, b_cstb = sb("cstb", [128, 640], BF16)
        scanm, b_scanm = sb("scanm", [128, 1024], BF16)
        pfm, b_pfm = sb("pfm", [128, L, NPF], F32)
        ptm, b_ptm = sb("ptm", [128, NPT], F32)
        esink, b_esink = sb("esink", [128, 16], F32)
        hT, b_hT = sb("hT", [128, 8, 128], BF16)
        kTr, b_kTr = [], []
        vr, b_vr = [], []
        for l in range(L):
            for j in range(2):
                t_, b_ = sb(f"kT{l}{j}", [64, 4, 128], BF16)
                kTr.append(t_); b_kTr.append(b_)
                t_, b_ = sb(f"v{l}{j}", [128, 4, 65], BF16)
                vr.append(t_); b_vr.append(b_)
        zcar, b_zcar = [], []
        ccar, b_ccar = [], []
        Hst, b_H = [], []
        for l in range(L):
            t_, b_ = sb(f"zcar{l}", [128, 27], F32); zcar.append(t_); b_zcar.append(b_)
            t_, b_ = sb(f"ccar{l}", [128, 44, 2], F32); ccar.append(t_); b_ccar.append(b_)
            t_, b_ = sb(f"H{l}", [128, 8, 64], F32); Hst.append(t_); b_H.append(b_)
        cos8, b_cos8 = sb("cos8", [128, 8, 32], F32)
        sin8, b_sin8 = sb("sin8", [128, 8, 32], F32)
        rtmp, b_rtmp = sb("rtmp", [128, 6, 32], F32)
        posi, b_posi = sb("posi", [128, 1], I32)
        small, b_small = sb("small", [128, 64], F32)
        pTf, b_pTf = sb("pTf", [128, L, 2, 128], F32)
        pTb, b_pTb = sb("pTb", [128, L, 2, 128], BF16)

        ARW = 24 * 1024
        arena = st.enter_context(nc.sbuf_tensor("arena", [128, ARW], F32))
        ptr = {}

        def aal(phase, name, words, dt=F32):
            o = ptr.get(phase, 0)
            ptr[phase] = o + words
            assert ptr[phase] <= ARW, (phase, name, ptr[phase])
            b = Buf(name, rng=(o, o + words))
            P.arena_bufs.append(b)
            v = arena[:, o:o + words]
            if dt == BF16:
                v = v.bitcast(BF16)
            return v, b

        def common(phase):
            d = {}
            d['scr'], d['b_scr'] = aal(phase, "scr", 1024)
            d['hn'], d['b_hn'] = aal(phase, "hn", 512, BF16)
            return d
        cm = common('all')
        for ph in ('att', 'rw', 'ffn', 'mrg'):
            ptr[ph] = ptr['all']
        scr, b_scr, hn, b_hn = cm['scr'], cm['b_scr'], cm['hn'], cm['b_hn']

        pst = [st.enter_context(nc.psum_tensor(f"ps{i}", [128, 1024], F32)) for i in range(4)]
        b_bank = [Buf(f"bank{i}") for i in range(8)]

        def bank(i):
            return pst[i // 2][:, (i % 2) * 512:(i % 2 + 1) * 512]

        def bankb(i):
            return bank(i).bitcast(BF16)

        identb = cstb[:, 0:128]
        mask_u = cstb[:, 128:256]
        mask_su = cstb[:, 256:384]
        mask_sl = cstb[:, 384:512]
        identf = cstf[:, CS_ID:CS_ID + 128]
        onesf = cstf[:, CS_ONES:CS_ONES + 128]
        invf = cstf[:, CS_IF:CS_IF + 32]

        P.dma('sp', DMA(cstf[:], cst_d[:, :]), b_cstf, writes=[b_cstf])
        P.dma('sp', DMA(pfm[:], pfm_d[:, :, :]), b_pfm, writes=[b_pfm])
        P.op('dve', CP(cstb[:], cstf[:, 0:640]), reads=[b_cstf], writes=[b_cstb])
        P.op('dve', CP(scanm[:], cstf[:, CS_SCAN:CS_SCAN + 1024]), reads=[b_cstf], writes=[b_scanm])
        for l in range(L):
            P.op('pool', MSET(zcar[l][:], 0.0), writes=[b_zcar[l]])
            P.op('pool', MSET(ccar[l][:], 0.0), writes=[b_ccar[l]])
            P.op('pool', MSET(Hst[l][:], 0.0), writes=[b_H[l]])
            for j in range(2):
                P.op('pool', MSET(vr[l * 2 + j][:], 1.0), writes=[b_vr[l * 2 + j]])
        b_wblk = [[Buf(f"wb{l}_{i}") for i in range(NB)] for l in range(L)]
        castown = [Buf(f"cast{i}") for i in range(8)]
        for l in range(L):
            for i, (name, off, n) in enumerate(blocks_meta):
                pc = 1024 if n % 1024 == 0 else 256
                src = w_d[l, :, off:off + n].rearrange("p (a b) -> p a b", b=pc)
                dst = wb_d[l, :, off:off + n].rearrange("p (a b) -> p a b", b=pc)
                flat = l * NB + i
                P.dma('pool', DMA(dst, src), castown[flat % 8], writes=[b_wblk[l][i]])
                if flat >= 6:
                    pl, pi = divmod(flat - 6, NB)
                    P.wait_all('pool', [b_wblk[pl][pi]])

        seq = [(t, l, i) for t in range(NT) for l in range(L) for i in range(NB)]
        state = {'issued': 0, 'next': 0}

        def issue_upto(q):
            while state['issued'] <= q and state['issued'] < len(seq):
                k = state['issued']
                t, l, i = seq[k]
                name, off, n = blocks_meta[i]
                slot = k % NSLOT
                P.dma('sp', DMA(ring[slot][:, 0:n], wb_d[l, :, off:off + n]), b_ring[slot],
                      reads=[b_wblk[l][i]], writes=[b_ring[slot]])
                state['issued'] = k + 1

        def wget(t, l, name):
            k = state['next']
            assert seq[k][0] == t and seq[k][1] == l and blocks_meta[seq[k][2]][0] == name, (seq[k], name)
            issue_upto(k + NSLOT - 1)
            state['next'] = k + 1
            slot = k % NSLOT
            return ring[slot], b_ring[slot]

        def rms_rstd(src_ap, src_bufs, out_col, eps=1e-6, n=1024.0):
            if isinstance(src_ap, list):
                for hI, a in enumerate(src_ap):
                    P.op('act', ACTF(scr[:, hI * 512:(hI + 1) * 512], a, AF.Square), reads=[src_bufs[hI]], writes=[b_scr])
            else:
                P.op('act', ACTF(scr[:, :], src_ap, AF.Square), reads=src_bufs, writes=[b_scr])
            P.op('dve', RSUM(small[:, out_col:out_col + 1], scr[:, :]), reads=[b_scr], writes=[b_small])
            P.op('act', ACTF(small[:, out_col:out_col + 1], small[:, out_col:out_col + 1], AF.Sqrt, scale=1.0 / n, bias=eps),
                 reads=[b_small], writes=[b_small])
            P.op('dve', RECIP(small[:, out_col:out_col + 1], small[:, out_col:out_col + 1]), reads=[b_small], writes=[b_small])

        def pre_norm_T(l, goff):
            rms_rstd(xt[:, :], [b_x], 0)
            P.op('dve', TS(hn[:, :], xt[:, :], small[:, 0:1], ALU.mult), reads=[b_x, b_small], writes=[b_hn])
            P.mm([TR(bankb(0)[:, j * 128:(j + 1) * 128], hn[:, j * 128:(j + 1) * 128], identb) for j in range(8)],
                 reads=[b_hn, b_cstb], writes=[b_bank[0]])
            g = pfm[:, l, goff:goff + 8].unsqueeze(2).to_broadcast([128, 8, 128])
            P.op('dve', TT(hT[:, :, :], bankb(0)[:, :].rearrange("p (a b) -> p a b", a=8), g, ALU.mult),
                 reads=[b_bank[0], b_pfm], writes=[b_hT])

        def post_norm_add(srcs, src_bufs, gcol):
            rms_rstd(srcs, src_bufs, 1)
            for hI in range(2):
                P.op('dve', STT(scr[:, hI * 512:(hI + 1) * 512], srcs[hI], small[:, 1:2],
                                ptm[:, gcol + hI * 512: gcol + (hI + 1) * 512], ALU.mult, ALU.mult),
                     reads=[src_bufs[hI], b_small, b_ptm], writes=[b_scr])
            P.op('pool', TT(xt[:, :], xt[:, :], scr[:, :], ALU.add), reads=[b_scr, b_x], writes=[b_x])

        def proj_tok(t, l, name, lhs, b_lhs, bk, K=8):
            blk, b_blk = wget(t, l, name)
            bv = blk[:, :].rearrange("p (a b) -> p a b", a=K)
            P.mm([MM(bank(bk), lhs[:, j, :], bv[:, j, :], start=(j == 0), stop=(j == K - 1)) for j in range(K)],
                 reads=[b_lhs, b_blk], writes=[b_bank[bk]])

        qr, b_qr = aal('att', "qr", 640, BF16)
        qT, b_qT = aal('att', "qT", 1024, BF16)
        pex, b_pex = [], []
        pm, b_pm = [], []
        for j in range(2):
            a_, b_ = aal('att', f"pex{j}", 256, BF16); pex.append(a_); b_pex.append(b_)
            a_, b_ = aal('att', f"pm{j}", 256, BF16); pm.append(a_); b_pm.append(b_)
        atok, b_atok = aal('att', "atok", 512, BF16)
        den, b_den = aal('att', "den", 16)
        AT, b_AT = sb("AT", [128, 8, 128], BF16)
        BT, b_BT = sb("BT", [128, 8, 128], BF16)
        zbuf, b_zbuf = aal('rw', "zbuf", 27 * 129)
        zs, b_zs = aal('rw', "zs", 27 * 128)
        NF = 6
        Fm, b_F = [], []
        for i in range(NF):
            a_, b_ = aal('rw', f"F{i}", 1024); Fm.append(a_); b_F.append(b_)
        la, b_la = aal('rw', "la", 256, BF16)
        ARt, b_AR = aal('rw', "AR", 1024, BF16)
        Btt, b_Bt = aal('rw', "Bt", 512, BF16)
        Ktt, b_Kt = aal('rw', "Kt", 512, BF16)
        Vtt, b_Vt = aal('rw', "Vt", 512, BF16)
        Btok, b_Btok = aal('rw', "Btok", 512, BF16)
        Ktok, b_Ktok = aal('rw', "Ktok", 512, BF16)
        Vtok, b_Vtok = aal('rw', "Vtok", 512, BF16)
        Hmid, b_Hmid = aal('rw', "Hmbd", 512, BF16)
        ARbd, b_ARbd = aal('rw', "ARbd", 2048, BF16)
        Hdec, b_Hdec = aal('rw', "Hdec", 512)
        ecol, b_ecol = aal('rw', "ecol", 24)
        MA, b_MA = aal('rw', "MA", 512, BF16)
        KA, b_KA = aal('rw', "KA", 512, BF16)
        Mf, b_Mf, Lf, b_Lf = [], [], [], []
        for j in range(2):
            a_, b_ = sb(f"Mf{j}", [128, 512], F32); Mf.append(a_[:, :]); b_Mf.append(b_)
            a_, b_ = sb(f"Lf{j}", [128, 512], F32); Lf.append(a_[:, :]); b_Lf.append(b_)
        Uf, b_Uf = [], []
        for j in range(2):
            a_, b_ = sb(f"Uf{j}", [128, 256], F32); Uf.append(a_[:, :]); b_Uf.append(b_)
        Ub, b_Ub = aal('rw', "Ub", 512, BF16)
        gst, b_gst = aal('rw', "gst", 64)
        ubuf, b_ubuf = aal('ffn', "ubuf", 4 * 130)
        cacc, b_cacc = aal('ffn', "cacc", 512)
        gel, b_gel = aal('ffn', "gel", 256)
        gmT, b_gmT = aal('ffn', "gmT", 22 * 64, BF16)
        sig, b_sig = [], []
        for j in range(2):
            a_, b_ = aal('mrg', f"sig{j}", 512); sig.append(a_); b_sig.append(b_)
        mixed, b_mixed = aal('mrg', "mixed", 1024)
        mixb, b_mixb = aal('mrg', "mixb", 512, BF16)
        mT, b_mT = aal('mrg', "mT", 512, BF16)

        qr3 = qr.rearrange("p (h d) -> p h d", h=20)
        qT3 = qT.rearrange("p (h t) -> p h t", h=16)
        atok3 = atok.rearrange("p (h d) -> p h d", h=16)
        zb3 = zbuf.rearrange("p (c t) -> p c t", c=27)
        zs3 = zs.rearrange("p (c t) -> p c t", c=27)
        F3 = [f.rearrange("p (a t) -> p a t", a=8) for f in Fm]
        la3 = la.rearrange("p (a t) -> p a t", a=4)
        AR4 = ARt.rearrange("p (a b t) -> p a b t", a=8, b=2)
        Bt3 = Btt.rearrange("p (a t) -> p a t", a=8)
        Kt3 = Ktt.rearrange("p (a t) -> p a t", a=8)
        Vt3 = Vtt.rearrange("p (a t) -> p a t", a=8)
        Hmbd4 = Hmid.rearrange("p (a h v) -> p a h v", a=8, h=2)
        ARbd5 = ARbd.rearrange("p (a h b t) -> p a h b t", a=8, h=2, b=2)
        Hdec3 = Hdec.rearrange("p (a v) -> p a v", a=8)
        MA4 = MA.rearrange("p (h b t) -> p h b t", h=4, b=2)
        KA4 = KA.rearrange("p (h b t) -> p h b t", h=4, b=2)
        Mf3 = [m.rearrange("p (h t) -> p h t", h=4) for m in Mf]
        Lf3 = [m.rearrange("p (h t) -> p h t", h=4) for m in Lf]
        Uf3 = [u.rearrange("p (h v) -> p h v", h=4) for u in Uf]
        Ub3 = Ub.rearrange("p (h v) -> p h v", h=16)
        ub3 = ubuf.rearrange("p (c t) -> p c t", c=4)
        cacc3 = cacc.rearrange("p (c t) -> p c t", c=4)
        gel3 = gel.rearrange("p (c t) -> p c t", c=2)
        gmT3 = gmT.rearrange("p (c t) -> p c t", c=22)
        mT3 = mT.rearrange("p (a t) -> p a t", a=8)

        def bc(ap, shape):
            return ap.to_broadcast(shape)

        def pcol(l, off, n=8):
            return pfm[:, l, off:off + n]

        try:
          for t in range(NT):
              tok0 = t * 128
              P.dma('sp', DMA(xt[:, :], x_d[tok0:tok0 + 128, :]), b_x, writes=[b_x])
              P.dma('sp', DMA(pTf[:, :, :, :], pT_d[:, :, :, tok0:tok0 + 128]), b_pTf, writes=[b_pTf])
              P.dma('sp', DMA(posi[:, :], pos_d[tok0:tok0 + 128, :]), b_posi, writes=[b_posi])
              P.op('pool', CP(pTb[:, :, :, :], pTf[:, :, :, :]), reads=[b_pTf], writes=[b_pTb])
              posf = small[:, 8:9]
              P.op('dve', CP(posf, posi[:, :]), reads=[b_posi], writes=[b_small])
              P.op('dve', TS(rtmp[:, 1, :], invf, posf, ALU.mult), reads=[b_cstf, b_small], writes=[b_rtmp])
              for which, shift, dst, b_dst in ((0, 0.0, sin8, b_sin8), (1, np.pi / 2, cos8, b_cos8)):
                  a_in = rtmp[:, 1, :]
                  if shift != 0.0:
                      P.op('dve', TS(rtmp[:, 4, :], rtmp[:, 1, :], float(shift), ALU.add), reads=[b_rtmp], writes=[b_rtmp])
                      a_in = rtmp[:, 4, :]
                  P.op('dve', TS(rtmp[:, 2, :], a_in, float(1.0 / TWO_PI), ALU.mult), reads=[b_rtmp], writes=[b_rtmp])
                  ki = rtmp[:, 5, :].bitcast(I32)
                  P.op('dve', CP(ki, rtmp[:, 2, :]), reads=[b_rtmp], writes=[b_rtmp])
                  P.op('dve', CP(rtmp[:, 2, :], ki), reads=[b_rtmp], writes=[b_rtmp])
                  P.op('dve', STT(rtmp[:, 3, :], rtmp[:, 2, :], -6.28125, a_in, ALU.mult, ALU.add), reads=[b_rtmp], writes=[b_rtmp])
                  P.op('dve', STT(rtmp[:, 3, :], rtmp[:, 2, :], float(-(TWO_PI - 6.28125)), rtmp[:, 3, :], ALU.mult, ALU.add),
                       reads=[b_rtmp], writes=[b_rtmp])
                  P.op('dve', TS(rtmp[:, 2, :], rtmp[:, 3, :], float(np.pi), ALU.is_gt, float(-TWO_PI), ALU.mult),
                       reads=[b_rtmp], writes=[b_rtmp])
                  P.op('dve', TT(rtmp[:, 3, :], rtmp[:, 3, :], rtmp[:, 2, :], ALU.add), reads=[b_rtmp], writes=[b_rtmp])
                  P.op('dve', TS(rtmp[:, 2, :], rtmp[:, 3, :], float(-np.pi), ALU.is_lt, float(TWO_PI), ALU.mult),
                       reads=[b_rtmp], writes=[b_rtmp])
                  P.op('dve', TT(rtmp[:, 3, :], rtmp[:, 3, :], rtmp[:, 2, :], ALU.add), reads=[b_rtmp], writes=[b_rtmp])
                  P.op('act', ACTF(rtmp[:, 5, :], rtmp[:, 3, :], AF.Sin), reads=[b_rtmp], writes=[b_rtmp])
                  P.op('dve', CP(dst[:, :, :], bc(rtmp[:, 5:6, :], [128, 8, 32])), reads=[b_rtmp], writes=[b_dst])

              for l in range(L):
                  cur = l * 2 + (t % 2)
                  prv = l * 2 + ((t + 1) % 2)
                  P.dma('sp', DMA(ptm[:, :], ptm_d[l, :, :]), b_ptm, writes=[b_ptm])
                  P.op('act', ACTF(esink[:, :], ptm[:, 3072:3088], AF.Exp), reads=[b_ptm], writes=[b_esink])
                  chk(1)
                  pre_norm_T(l, PF_GMIX)
                  chk(2)
                  for cb in range(3):
                      proj_tok(t, l, f"qkv{cb}", hT, b_hT, 1 + cb)
                  P.op('act', ACTF(vr[cur][:, :, 0:64], bank(3)[:, 256:512].rearrange("p (h d) -> p h d", h=4), AF.Copy),
                       reads=[b_bank[3]], writes=[b_vr[cur]])
                  for (bk, nh, h0, c0_) in ((1, 8, 0, 0), (2, 8, 8, 0), (3, 4, 16, 0)):
                      src = bank(bk)[:, 0:nh * 64].rearrange("p (h d) -> p h d", h=nh)
                      x1 = src[:, :, 0:32]
                      x2 = src[:, :, 32:64]
                      cs_ = cos8[:, 0:nh, :]
                      sn_ = sin8[:, 0:nh, :]
                      s4 = scr[:, :].rearrange("p (k h d) -> p k h d", k=4, h=8)
                      P.op('dve', TT(s4[:, 0, 0:nh, :], x1, cs_, ALU.mult), reads=[b_bank[bk], b_cos8], writes=[b_scr])
                      P.op('dve', TT(s4[:, 1, 0:nh, :], x2, sn_, ALU.mult), reads=[b_bank[bk], b_sin8], writes=[b_scr])
                      P.op('dve', TT(s4[:, 2, 0:nh, :], x2, cs_, ALU.mult), reads=[b_bank[bk], b_cos8], writes=[b_scr])
                      P.op('dve', TT(s4[:, 3, 0:nh, :], x1, sn_, ALU.mult), reads=[b_bank[bk], b_sin8], writes=[b_scr])
                      P.op('pool', TT(qr3[:, h0:h0 + nh, 0:32], s4[:, 0, 0:nh, :], s4[:, 1, 0:nh, :], ALU.subtract),
                           reads=[b_scr], writes=[b_qr])
                      P.op('pool', TT(qr3[:, h0:h0 + nh, 32:64], s4[:, 2, 0:nh, :], s4[:, 3, 0:nh, :], ALU.add),
                           reads=[b_scr], writes=[b_qr])
                  for (bk, h0, nh) in ((4, 0, 8), (5, 8, 8), (6, 16, 4)):
                      P.mm([TR(bankb(bk)[0:64, j * 128:(j + 1) * 128], qr3[:, h0 + j, :], identb) for j in range(nh)],
                           reads=[b_qr, b_cstb], writes=[b_bank[bk]])
                  P.op('act', ACTF(qT3[0:64, 0:8, :], bankb(4)[0:64, :].rearrange("p (h t) -> p h t", h=8), AF.Copy),
                       reads=[b_bank[4]], writes=[b_qT])
                  P.op('dve', CP(qT3[0:64, 8:16, :], bankb(5)[0:64, :].rearrange("p (h t) -> p h t", h=8)),
                       reads=[b_bank[5]], writes=[b_qT])
                  P.op('act', ACTF(kTr[cur][:, :, :], bankb(6)[0:64, 0:512].rearrange("p (h t) -> p h t", h=4), AF.Copy),
                       reads=[b_bank[6]], writes=[b_kTr[cur]])
                  chk(3)
                  for g in range(4):
                      bs_c, bs_p, bo = (0, 1, 2) if g % 2 == 0 else (3, 4, 5)
                      rhs_q = qT[0:64, g * 512:(g + 1) * 512]
                      P.mm([MM(bank(bs_c), kTr[cur][:, g, :], rhs_q)], reads=[b_kTr[cur], b_qT], writes=[b_bank[bs_c]])
                      P.op('act', ACTF(pex[0][:, :], bank(bs_c), AF.Exp, scale=0.125), reads=[b_bank[bs_c]], writes=[b_pex[0]])
                      P.op('pool', TT(pm[0].rearrange("p (h t) -> p h t", h=4), pex[0].rearrange("p (h t) -> p h t", h=4),
                                      bc(mask_u.unsqueeze(1), [128, 4, 128]), ALU.mult),
                           reads=[b_pex[0], b_cstb], writes=[b_pm[0]])
                      if t > 0:
                          P.mm([MM(bank(bs_p), kTr[prv][:, g, :], rhs_q)], reads=[b_kTr[prv], b_qT], writes=[b_bank[bs_p]])
                          P.op('act', ACTF(pex[1][:, :], bank(bs_p), AF.Exp, scale=0.125), reads=[b_bank[bs_p]], writes=[b_pex[1]])
                          P.op('pool', TT(pm[1].rearrange("p (h t) -> p h t", h=4), pex[1].rearrange("p (h t) -> p h t", h=4),
                                          bc(mask_sl.unsqueeze(1), [128, 4, 128]), ALU.mult),
                               reads=[b_pex[1], b_cstb], writes=[b_pm[1]])
                      fns = []
                      for hq in range(4):
                          o_ = bank(bo)[:, hq * 65:(hq + 1) * 65]
                          if t > 0:
                              fns.append(MM(o_, pm[1][:, hq * 128:(hq + 1) * 128], vr[prv][:, g, :], start=True, stop=False))
                              fns.append(MM(o_, pm[0][:, hq * 128:(hq + 1) * 128], vr[cur][:, g, :], start=False, stop=True))
                          else:
                              fns.append(MM(o_, pm[0][:, hq * 128:(hq + 1) * 128], vr[cur][:, g, :], start=True, stop=True))
                      rd = [b_pm[0], b_vr[cur]] + ([b_pm[1], b_vr[prv]] if t > 0 else [])
                      P.mm(fns, reads=rd, writes=[b_bank[bo]])
                      o3 = bank(bo)[:, 0:260].rearrange("p (h d) -> p h d", h=4)
                      P.op('dve', TT(den[:, 0:4].unsqueeze(2), o3[:, :, 64:65], esink[:, g * 4:(g + 1) * 4].unsqueeze(2), ALU.add),
                           reads=[b_bank[bo], b_esink], writes=[b_den])
                      P.op('dve', RECIP(den[:, 4:8], den[:, 0:4]), reads=[b_den], writes=[b_den])
                      P.op('dve', TT(atok3[:, g * 4:(g + 1) * 4, :], o3[:, :, 0:64], bc(den[:, 4:8].unsqueeze(2), [128, 4, 64]), ALU.mult),
                           reads=[b_bank[bo], b_den], writes=[b_atok])
                  P.mm([TR(bankb(6)[:, j * 128:(j + 1) * 128], atok[:, j * 128:(j + 1) * 128], identb) for j in range(8)],
                       reads=[b_atok, b_cstb], writes=[b_bank[6]])
                  P.op('act', ACTF(AT[:, :, :], bankb(6)[:, :].rearrange("p (a t) -> p a t", a=8), AF.Copy),
                       reads=[b_bank[6]], writes=[b_AT])

                  chk(4)
                  for rb in range(6):
                      blk, b_blk = wget(t, l, f"rw{rb}")
                      bv = blk[:, :].rearrange("p (a b) -> p a b", a=8)
                      bk = rb % 2
                      fns = []
                      for fc in range(4):
                          for j in range(8):
                              fns.append(MM(bank(bk)[:, fc * 128:(fc + 1) * 128], bv[:, j, fc * 128:(fc + 1) * 128], hT[:, j, :],
                                            start=(j == 0), stop=(j == 7)))
                      P.mm(fns, reads=[b_hT, b_blk], writes=[b_bank[bk]])
                      P.op('act', ACTF(zb3[:, rb * 4:(rb + 1) * 4, 1:129], bank(bk).rearrange("p (c t) -> p c t", c=4), AF.Copy),
                           reads=[b_bank[bk]], writes=[b_zbuf])
                  blk, b_blk = wget(t, l, "lora1")
                  bv = blk[:, 0:2304].rearrange("p (a b) -> p a b", a=8)
                  fns = []
                  for fc, (c0_, cn) in enumerate(((0, 128), (128, 128), (256, 32))):
                      for j in range(8):
                          fns.append(MM(bank(2)[0:cn, fc * 128:(fc + 1) * 128], bv[:, j, c0_:c0_ + cn], hT[:, j, :],
                                        start=(j == 0), stop=(j == 7)))
                  P.mm(fns, reads=[b_hT, b_blk], writes=[b_bank[2]])
                  P.op('act', ACTF(zb3[:, 24:26, 1:129], bank(2)[:, 0:256].rearrange("p (c t) -> p c t", c=2), AF.Copy),
                       reads=[b_bank[2]], writes=[b_zbuf])
                  P.op('pool', MSET(zb3[:, 26, :], 0.0), writes=[b_zbuf])
                  P.op('act', ACTF(zb3[0:32, 26, 1:129], bank(2)[0:32, 256:384], AF.Copy), reads=[b_bank[2]], writes=[b_zbuf])
                  chk(5)
                  P.op('dve', CP(zb3[:, :, 0:1], zcar[l][:, :].unsqueeze(2)), reads=[b_zcar[l]], writes=[b_zbuf])
                  P.op('dve', TT(zs3[:, :, :], zb3[:, :, 0:128], zb3[:, :, 1:129], ALU.subtract), reads=[b_zbuf], writes=[b_zs])
                  P.op('pool', TT(zs3[:, :, :], zs3[:, :, :], bc(pfm[:, l, PF_MU:PF_MU + 27].unsqueeze(2), [128, 27, 128]), ALU.mult),
                       reads=[b_zs, b_pfm], writes=[b_zs])
                  P.op('dve', TT(zs3[:, :, :], zs3[:, :, :], zb3[:, :, 1:129], ALU.add), reads=[b_zs, b_zbuf], writes=[b_zs])
                  P.op('pool', CP(zcar[l][:, :].unsqueeze(2), zb3[:, :, 128:129]), reads=[b_zbuf], writes=[b_zcar[l]])
                  r3 = zs3[:, 0:8, :]
                  k3 = zs3[:, 8:16, :]
                  v3 = zs3[:, 16:24, :]
                  chk(6)
                  P.op('pool', MSET(la[:, 0:256], 0.0), writes=[b_la])
                  P.op('act', ACTF(la3[0:64, 0, :], zs3[0:64, 24, :], AF.Tanh), reads=[b_zs], writes=[b_la])
                  P.op('act', ACTF(la3[64:128, 1, :], zs3[64:128, 24, :], AF.Copy), reads=[b_zs], writes=[b_la])
                  P.op('act', ACTF(la3[:, 2, :], zs3[:, 25, :], AF.Sigmoid), reads=[b_zs], writes=[b_la])
                  P.op('act', ACTF(la3[:, 3, :], zs3[:, 26, :], AF.Sigmoid), reads=[b_zs], writes=[b_la])
                  chk(61)
                  blk, b_blk = wget(t, l, "lora2")
                  l2v = blk[:, 0:3072].rearrange("p (a b) -> p a b", a=3)
                  fw, fa, fg = [], [], []
                  for hp in range(8):
                      bko = hp // 4
                      cs_ = slice((hp % 4) * 128, (hp % 4 + 1) * 128)
                      fw.append(MM(bank(0 + bko)[:, cs_], l2v[:, 0, hp * 128:(hp + 1) * 128], la3[:, 0, :]))
                      fa.append(MM(bank(2 + bko)[:, cs_], l2v[:, 0, hp * 128:(hp + 1) * 128], la3[:, 1, :]))
                      fg.append(MM(bank(4 + bko)[:, cs_], l2v[:, 1, hp * 128:(hp + 1) * 128], la3[:, 2, :], start=True, stop=False))
                      fg.append(MM(bank(4 + bko)[:, cs_], l2v[:, 2, hp * 128:(hp + 1) * 128], la3[:, 3, :], start=False, stop=True))
                  P.mm(fw, reads=[b_la, b_blk], writes=[b_bank[0], b_bank[1]])
                  P.mm(fa, reads=[b_la, b_blk], writes=[b_bank[2], b_bank[3]])
                  P.mm(fg, reads=[b_la, b_blk], writes=[b_bank[4], b_bank[5]])

                  def psum2(i):
                      return pst[i][:, :].rearrange("p (a t) -> p a t", a=8)
                  chk(62)
                  P.op('dve', TT(F3[0], psum2(0), bc(pcol(l, PF_W0).unsqueeze(2), [128, 8, 128]), ALU.add),
                       reads=[b_bank[0], b_bank[1], b_pfm], writes=[b_F[0]])
                  P.op('act', ACTF(Fm[0], Fm[0], AF.Sigmoid), reads=[b_F[0]], writes=[b_F[0]])
                  P.op('dve', TT(F3[1], psum2(1), bc(pcol(l, PF_A0).unsqueeze(2), [128, 8, 128]), ALU.add),
                       reads=[b_bank[2], b_bank[3], b_pfm], writes=[b_F[1]])
                  P.op('act', ACTF(Fm[1], Fm[1], AF.Sigmoid), reads=[b_F[1]], writes=[b_F[1]])
                  chk(63)
                  P.op('dve', lambda e: e.tensor_tensor_scan(out=Fm[2], data0=scanm[:, :], data1=Fm[0], initial=0.0,
                                                             op0=ALU.mult, op1=ALU.add),
                       reads=[b_scanm, b_F[0]], writes=[b_F[2]])
                  chk(64)
                  ec3 = ecol.rearrange("p (k a) -> p k a", k=3)
                  P.op('act', ACTF(ec3[:, 0, :].unsqueeze(2), F3[2][:, :, 63:64], AF.Exp, scale=-C0), reads=[b_F[2]], writes=[b_ecol])
                  P.op('act', ACTF(ec3[:, 1, :].unsqueeze(2), F3[2][:, :, 127:128], AF.Exp, scale=-C0), reads=[b_F[2]], writes=[b_ecol])
                  P.op('pool', MSET(Hmid[:, :], 0.0), writes=[b_Hmid])
                  for half in range(2):
                      ps_ = slice(half * 64, (half + 1) * 64)
                      P.op('dve', TT(Hmbd4[ps_, :, half, :], Hst[l][ps_, :, :], bc(ec3[ps_, 0, :].unsqueeze(2), [64, 8, 64]), ALU.mult),
                           reads=[b_H[l], b_ecol], writes=[b_Hmid])
                  P.op('pool', TT(Hdec3, Hst[l][:, :, :], bc(ec3[:, 1, :].unsqueeze(2), [128, 8, 64]), ALU.mult),
                       reads=[b_H[l], b_ecol], writes=[b_Hdec])
                  P.op('dve', CP(gst[:, 0:8].unsqueeze(2), F3[2][:, :, 63:64]), reads=[b_F[2]], writes=[b_gst])
                  P.op('dve', TT(F3[2], F3[2], bc(gst[:, 0:8].unsqueeze(2), [128, 8, 128]), ALU.subtract),
                       reads=[b_F[2], b_gst], writes=[b_F[2]])
                  P.op('pool', TT(Fm[0], Fm[2], Fm[0], ALU.subtract), reads=[b_F[2], b_F[0]], writes=[b_F[0]])
                  P.op('act', ACTF(ec3[:, 2, :].unsqueeze(2), F3[2][:, :, 127:128], AF.Exp, scale=-C0), reads=[b_F[2]], writes=[b_ecol])
                  P.op('act', ACTF(Fm[0], Fm[0], AF.Exp, scale=-C0), reads=[b_F[0]], writes=[b_F[0]])
                  P.op('act', ACTF(Fm[3], Fm[2], AF.Exp, scale=-C0), reads=[b_F[2]], writes=[b_F[3]])
                  P.op('act', ACTF(Fm[2], Fm[2], AF.Exp, scale=C0), reads=[b_F[2]], writes=[b_F[2]])
                  P.op('dve', TT(AR4[:, :, 1, :], r3, F3[3], ALU.mult), reads=[b_zs, b_F[3]], writes=[b_AR])
                  chk(65)
                  P.op('dve', TT(F3[4], k3, bc(pcol(l, PF_KK).unsqueeze(2), [128, 8, 128]), ALU.mult),
                       reads=[b_zs, b_pfm], writes=[b_F[4]])
                  P.op('act', ACTF(Fm[5], Fm[4], AF.Square), reads=[b_F[4]], writes=[b_F[5]])
                  P.mm([MM(bank(6 + hp // 4)[:, (hp % 4) * 128:(hp % 4 + 1) * 128], onesf, F3[5][:, hp, :]) for hp in range(8)],
                       reads=[b_F[5], b_cstf], writes=[b_bank[6], b_bank[7]])
                  P.op('act', ACTF(Fm[5], pst[3][:, :], AF.Sqrt), reads=[b_bank[6], b_bank[7]], writes=[b_F[5]])
                  P.op('dve', TS(Fm[5], Fm[5], 1e-12, ALU.max), reads=[b_F[5]], writes=[b_F[5]])
                  P.op('dve', RECIP(Fm[5], Fm[5]), reads=[b_F[5]], writes=[b_F[5]])
                  P.op('pool', TT(Fm[4], Fm[4], Fm[5], ALU.mult), reads=[b_F[4], b_F[5]], writes=[b_F[4]])
                  chk(66)
                  P.op('dve', STT(AR4[:, :, 0, :], F3[4], -1.0, F3[0], ALU.mult, ALU.mult), reads=[b_F[4], b_F[0]], writes=[b_AR])
                  P.op('pool', TT(Fm[5], Fm[4], Fm[1], ALU.mult), reads=[b_F[4], b_F[1]], writes=[b_F[5]])
                  P.op('dve', TT(Btt, Fm[5], Fm[2], ALU.mult), reads=[b_F[5], b_F[2]], writes=[b_Bt])
                  P.op('dve', TS(Fm[5], Fm[1], -1.0, ALU.add), reads=[b_F[1]], writes=[b_F[5]])
                  P.op('pool', TT(F3[5], F3[5], bc(pcol(l, PF_KA).unsqueeze(2), [128, 8, 128]), ALU.mult),
                       reads=[b_F[5], b_pfm], writes=[b_F[5]])
                  P.op('dve', STT(F3[4], F3[5], 1.0, k3, ALU.add, ALU.mult), reads=[b_F[5], b_zs], writes=[b_F[4]])
                  P.op('pool', TT(Ktt, Fm[4], Fm[2], ALU.mult), reads=[b_F[4], b_F[2]], writes=[b_Kt])
                  P.op('act', ACTF(Vt3, v3, AF.Copy), reads=[b_zs], writes=[b_Vt])
                  P.op('dve', TT(F3[5], r3, F3[4], ALU.mult), reads=[b_zs, b_F[4]], writes=[b_F[5]])
                  P.op('pool', TT(F3[5], F3[5], bc(pcol(l, PF_RK).unsqueeze(2), [128, 8, 128]), ALU.mult),
                       reads=[b_F[5], b_pfm], writes=[b_F[5]])
                  P.mm([MM(bank(6 + hp // 4)[:, (hp % 4) * 128:(hp % 4 + 1) * 128], onesf, F3[5][:, hp, :]) for hp in range(8)],
                       reads=[b_F[5], b_cstf], writes=[b_bank[6], b_bank[7]])
                  P.op('dve', TT(F3[5], psum2(3), v3, ALU.mult), reads=[b_bank[6], b_bank[7], b_zs], writes=[b_F[5]])
                  P.op('act', ACTF(F3[1], psum2(2), AF.Copy), reads=[b_bank[4], b_bank[5]], writes=[b_F[1]])
                  chk(67)
                  for (src3, b_src, bk, dstt, b_dstt, eng) in ((Kt3, b_Kt, 0, Ktok, b_Ktok, 'act'), (Bt3, b_Bt, 1, Btok, b_Btok, 'dve'),
                                                               (Vt3, b_Vt, 2, Vtok, b_Vtok, 'act')):
                      P.mm([TR(bankb(bk)[:, hp * 128:(hp + 1) * 128], src3[:, hp, :], identb) for hp in range(8)],
                           reads=[b_src, b_cstb], writes=[b_bank[bk]])
                      if eng == 'act':
                          P.op('act', ACTF(dstt, bankb(bk)[:, :], AF.Copy), reads=[b_bank[bk]], writes=[b_dstt])
                      else:
                          P.op('dve', CP(dstt, bankb(bk)[:, :]), reads=[b_bank[bk]], writes=[b_dstt])
                  chk(7)
                  P.op('pool', MSET(ARbd[:, :], 0.0), writes=[b_ARbd])
                  P.op('dve', CP(ARbd5[0:64, :, 0, :, :], AR4[0:64, :, :, :]), reads=[b_AR], writes=[b_ARbd])
                  P.op('pool', CP(ARbd5[64:128, :, 1, :, :], AR4[64:128, :, :, :]), reads=[b_AR], writes=[b_ARbd])
                  for g in range(4):
                      f1, f2 = [], []
                      for i in range(2):
                          hp = 2 * g + i
                          f1.append(MM(bank(0 + i), Bt3[:, hp, :], ARbd[:, hp * 512:(hp + 1) * 512]))
                          f2.append(MM(bank(2 + i), Kt3[:, hp, :], ARbd[:, hp * 512:(hp + 1) * 512]))
                      P.mm(f1, reads=[b_Bt, b_ARbd], writes=[b_bank[0], b_bank[1]])
                      P.mm(f2, reads=[b_Kt, b_ARbd], writes=[b_bank[2], b_bank[3]])
                      pa4 = pst[0][:, :].rearrange("p (h b t) -> p h b t", h=4, b=2)
                      pb4 = pst[1][:, :].rearrange("p (h b t) -> p h b t", h=4, b=2)
                      P.op('dve', TT(Mf3[0].bitcast(F32R), pa4[:, :, 0, :], bc(mask_su.unsqueeze(1), [128, 4, 128]), ALU.mult),
                           reads=[b_bank[0], b_bank[1], b_cstb], writes=[b_Mf[0]])
                      P.op('dve', TT(MA4[:, :, 1, :], pa4[:, :, 1, :], bc(mask_u.unsqueeze(1), [128, 4, 128]), ALU.mult),
                           reads=[b_bank[0], b_bank[1], b_cstb], writes=[b_MA])
                      P.op('dve', TT(KA4[:, :, 0, :], pb4[:, :, 0, :], bc(mask_su.unsqueeze(1), [128, 4, 128]), ALU.mult),
                           reads=[b_bank[2], b_bank[3], b_cstb], writes=[b_KA])
                      P.op('dve', TT(KA4[:, :, 1, :], pb4[:, :, 1, :], bc(mask_u.unsqueeze(1), [128, 4, 128]), ALU.mult),
                           reads=[b_bank[2], b_bank[3], b_cstb], writes=[b_KA])
                      P.mm([TR(bank(4)[:, hh * 128:(hh + 1) * 128], Mf3[0][:, hh, :], identf) for hh in range(4)],
                           reads=[b_Mf[0], b_cstf], writes=[b_bank[4]])
                      P.op('act', ACTF(Lf[0].bitcast(F32R), bank(4), AF.Copy), reads=[b_bank[4]], writes=[b_Lf[0]])
                      fns = []
                      for i in range(2):
                          hp = 2 * g + i
                          fns.append(MM(bank(5)[:, i * 128:(i + 1) * 128], AR4[:, hp, 0, :], Hmid[:, hp * 128:(hp + 1) * 128], start=True, stop=False))
                          for j in range(2):
                              hh = 2 * i + j
                              h = 4 * g + hh
                              fns.append(MM(bank(5)[:, hh * 64:(hh + 1) * 64], KA4[:, hh, 0, :], Vtok[:, h * 64:(h + 1) * 64], start=False, stop=(j == 1)))
                      P.mm(fns, reads=[b_AR, b_Hmid, b_KA, b_Vtok], writes=[b_bank[5]])
                      P.op('act', ACTF(Uf[0].bitcast(F32R), bank(5)[:, 0:256], AF.Copy), reads=[b_bank[5]], writes=[b_Uf[0]])
                      for k in range(7):
                          ci, ni = k % 2, (k + 1) % 2
                          P.mm([MMR(bank(5)[:, hh * 64:(hh + 1) * 64], Mf3[ci][:, hh, :], Uf3[ci][:, hh, :]) for hh in range(4)],
                               reads=[b_Mf[ci], b_Uf[ci]], writes=[b_bank[5]])
                          P.op('dve', TT(Uf[ni].bitcast(F32R), bank(5)[:, 0:256], Uf[ci], ALU.add), reads=[b_bank[5], b_Uf[ci]], writes=[b_Uf[ni]])
                          if k < 6:
                              P.mm([MMR(bank(6)[:, hh * 128:(hh + 1) * 128], Lf3[ci][:, hh, :], Mf3[ci][:, hh, :]) for hh in range(4)],
                                   reads=[b_Mf[ci], b_Lf[ci]], writes=[b_bank[6]])
                              P.op('act', ACTF(Mf[ni].bitcast(F32R), bank(6), AF.Copy), reads=[b_bank[6]], writes=[b_Mf[ni]])
                          if k < 5:
                              P.mm([MMR(bank(7)[:, hh * 128:(hh + 1) * 128], Mf3[ci][:, hh, :], Lf3[ci][:, hh, :]) for hh in range(4)],
                                   reads=[b_Mf[ci], b_Lf[ci]], writes=[b_bank[7]])
                              P.op('dve', CP(Lf[ni].bitcast(F32R), bank(7)), reads=[b_bank[7]], writes=[b_Lf[ni]])
                      P.op('act', ACTF(Ub3[:, 4 * g:4 * g + 4, :], Uf3[1], AF.Copy), reads=[b_Uf[1]], writes=[b_Ub])
                      fns = []
                      for i in range(2):
                          hp = 2 * g + i
                          fns.append(MM(bank(4)[:, i * 128:(i + 1) * 128], AR4[:, hp, 1, :], Hmid[:, hp * 128:(hp + 1) * 128], start=True, stop=False))
                          for j in range(2):
                              hh = 2 * i + j
                              h = 4 * g + hh
                              o_ = bank(4)[:, hh * 64:(hh + 1) * 64]
                              fns.append(MM(o_, MA4[:, hh, 1, :], Ub3[:, h, :], start=False, stop=False))
                              fns.append(MM(o_, KA4[:, hh, 1, :], Vtok[:, h * 64:(h + 1) * 64], start=False, stop=(j == 1)))
                      P.mm(fns, reads=[b_AR, b_Hmid, b_MA, b_Ub, b_KA, b_Vtok], writes=[b_bank[4]])
                      P.op('act', ACTF(scr[:, g * 256:(g + 1) * 256], bank(4)[:, 0:256], AF.Copy), reads=[b_bank[4]], writes=[b_scr])
                  chk(8)
                  fns = []
                  for hp in range(8):
                      o_ = bank(hp // 4)[:, (hp % 4) * 128:(hp % 4 + 1) * 128]
                      fns.append(MM(o_, Btok[:, hp * 128:(hp + 1) * 128], Ub[:, hp * 128:(hp + 1) * 128], start=True, stop=False))
                      fns.append(MM(o_, Ktok[:, hp * 128:(hp + 1) * 128], Vtok[:, hp * 128:(hp + 1) * 128], start=False, stop=True))
                  P.mm(fns, reads=[b_Btok, b_Ub, b_Ktok, b_Vtok], writes=[b_bank[0], b_bank[1]])
                  dH = psum2(0)
                  for half in range(2):
                      ps_ = slice(half * 64, (half + 1) * 64)
                      P.op('dve', TT(Hst[l][ps_, :, :], dH[ps_, :, half * 64:(half + 1) * 64],
                                     bc(ec3[ps_, 2, :].unsqueeze(2), [64, 8, 64]), ALU.mult),
                           reads=[b_bank[0], b_bank[1], b_ecol], writes=[b_H[l]])
                  P.op('pool', TT(Hst[l][:, :, :], Hst[l][:, :, :], Hdec3, ALU.add), reads=[b_H[l], b_Hdec], writes=[b_H[l]])
                  y3 = scr[:, :].rearrange("p (h v) -> p h v", h=16)
                  P.op('dve', RSUM(gst[:, 0:16], y3), reads=[b_scr], writes=[b_gst])
                  P.op('dve', TS(gst[:, 0:16], gst[:, 0:16], 1.0 / 64, ALU.mult), reads=[b_gst], writes=[b_gst])
                  P.op('dve', TT(y3, y3, bc(gst[:, 0:16].unsqueeze(2), [128, 16, 64]), ALU.subtract), reads=[b_scr, b_gst], writes=[b_scr])
                  F43 = Fm[4].rearrange("p (h v) -> p h v", h=16)
                  P.op('act', ACTF(Fm[4], scr[:, :], AF.Square), reads=[b_scr], writes=[b_F[4]])
                  P.op('dve', RSUM(gst[:, 16:32], F43), reads=[b_F[4]], writes=[b_gst])
                  P.op('act', ACTF(gst[:, 16:32], gst[:, 16:32], AF.Sqrt, scale=1.0 / 64, bias=64e-5), reads=[b_gst], writes=[b_gst])
                  P.op('dve', RECIP(gst[:, 16:32], gst[:, 16:32]), reads=[b_gst], writes=[b_gst])
                  P.op('dve', TT(y3, y3, bc(gst[:, 16:32].unsqueeze(2), [128, 16, 64]), ALU.mult), reads=[b_scr, b_gst], writes=[b_scr])
                  P.mm([TR(bank(2 + hp // 4)[:, (hp % 4) * 128:(hp % 4 + 1) * 128], scr[:, hp * 128:(hp + 1) * 128], identf) for hp in range(8)],
                       reads=[b_scr, b_cstf], writes=[b_bank[2], b_bank[3]])
                  P.op('dve', TT(F3[4], psum2(1), bc(pcol(l, PF_GNW).unsqueeze(2), [128, 8, 128]), ALU.mult),
                       reads=[b_bank[2], b_bank[3], b_pfm], writes=[b_F[4]])
                  P.op('pool', TT(F3[4], F3[4], bc(pcol(l, PF_GNB).unsqueeze(2), [128, 8, 128]), ALU.add), reads=[b_F[4], b_pfm], writes=[b_F[4]])
                  P.op('dve', TT(Fm[4], Fm[4], Fm[5], ALU.add), reads=[b_F[4], b_F[5]], writes=[b_F[4]])
                  P.op('pool', TT(BT[:, :, :], F3[4], F3[1], ALU.mult), reads=[b_F[4], b_F[1]], writes=[b_BT])

                  chk(9)
                  for br, (XT_, b_XT) in enumerate(((AT, b_AT), (BT, b_BT))):
                      for cb in range(2):
                          proj_tok(t, l, f"g{br}{cb}", hT, b_hT, 0)
                          proj_tok(t, l, f"wo{br}{cb}", XT_, b_XT, 1)
                          P.op('act', ACTF(sig[cb], bank(0), AF.Sigmoid), reads=[b_bank[0]], writes=[b_sig[cb]])
                          dstm = mixed[:, cb * 512:(cb + 1) * 512]
                          if br == 0:
                              P.op('dve', TT(dstm, bank(1), sig[cb], ALU.mult), reads=[b_bank[1], b_sig[cb]], writes=[b_mixed])
                          else:
                              P.op('dve', TT(sig[cb], bank(1), sig[cb], ALU.mult), reads=[b_bank[1], b_sig[cb]], writes=[b_sig[cb]])
                              P.op('pool', TT(mixb[:, cb * 512:(cb + 1) * 512], dstm, sig[cb], ALU.add),
                                   reads=[b_mixed, b_sig[cb]], writes=[b_mixb])
                  P.mm([TR(bankb(2)[:, j * 128:(j + 1) * 128], mixb[:, j * 128:(j + 1) * 128], identb) for j in range(8)],
                       reads=[b_mixb, b_cstb], writes=[b_bank[2]])
                  P.op('act', ACTF(mT3, bankb(2)[:, :].rearrange("p (a t) -> p a t", a=8), AF.Copy), reads=[b_bank[2]], writes=[b_mT])
                  for cb in range(2):
                      proj_tok(t, l, f"wout{cb}", mT3, b_mT, 3 + cb)
                  post_norm_add([bank(3), bank(4)], [b_bank[3], b_bank[4]], 0)

                  chk(10)
                  pre_norm_T(l, PF_GFFN)
                  for i in range(11):
                      blk, b_blk = wget(t, l, f"up{i}")
                      bv = blk[:, :].rearrange("p (a b) -> p a b", a=8)
                      bk = 5 + (i % 2)
                      fns = []
                      for c4 in range(4):
                          for j in range(8):
                              fns.append(MM(bank(bk)[:, c4 * 128:(c4 + 1) * 128], bv[:, j, c4 * 128:(c4 + 1) * 128], hT[:, j, :],
                                            start=(j == 0), stop=(j == 7)))
                      P.mm(fns, reads=[b_hT, b_blk], writes=[b_bank[bk]])
                      gcs = [2 * i, 22 + 2 * i, 2 * i + 1, 22 + 2 * i + 1]
                      psu = bank(bk).rearrange("p (c t) -> p c t", c=4)
                      P.op('act', ACTF(ub3[:, :, 2:130], psu, AF.Copy), reads=[b_bank[bk]], writes=[b_ubuf])
                      for c4, gc in enumerate(gcs):
                          P.op('pool', CP(ub3[:, c4, 0:2], ccar[l][:, gc, :]), reads=[b_ccar[l]], writes=[b_ubuf])
                      for c4, gc in enumerate(gcs):
                          P.op('pool', CP(ccar[l][:, gc, :], ub3[:, c4, 128:130]), reads=[b_ubuf], writes=[b_ccar[l]])
                      for c4, gc in enumerate(gcs):
                          w2c = pfm[:, l, PF_CW + 2 * 44 + gc: PF_CW + 2 * 44 + gc + 1]
                          w1c = pfm[:, l, PF_CW + 1 * 44 + gc: PF_CW + 1 * 44 + gc + 1]
                          w0c = pfm[:, l, PF_CW + 0 * 44 + gc: PF_CW + 0 * 44 + gc + 1]
                          bcv = pfm[:, l, PF_CB + gc: PF_CB + gc + 1]
                          P.op('dve', TS(cacc3[:, c4, :], ub3[:, c4, 2:130], w2c, ALU.mult, bcv, ALU.add),
                               reads=[b_ubuf, b_pfm], writes=[b_cacc])
                          P.op('dve', STT(cacc3[:, c4, :], ub3[:, c4, 1:129], w1c, cacc3[:, c4, :], ALU.mult, ALU.add),
                               reads=[b_ubuf, b_pfm, b_cacc], writes=[b_cacc])
                          P.op('dve', STT(cacc3[:, c4, :], ub3[:, c4, 0:128], w0c, cacc3[:, c4, :], ALU.mult, ALU.add),
                               reads=[b_ubuf, b_pfm, b_cacc], writes=[b_cacc])
                      ca = cacc.rearrange("p (i ab t) -> p i ab t", i=2, ab=2)
                      P.op('act', ACTF(gel3, ca[:, :, 0, :], AF.Gelu_apprx_tanh), reads=[b_cacc], writes=[b_gel])
                      P.op('pool', TT(gmT3[:, 2 * i:2 * i + 2, :], gel3, ca[:, :, 1, :], ALU.mult), reads=[b_gel, b_cacc], writes=[b_gmT])
                  for db in range(6):
                      nj = 4 if db < 5 else 2
                      blk, b_blk = wget(t, l, f"down{db}")
                      bv = blk[:, 0:nj * 1024].rearrange("p (a b) -> p a b", a=nj)
                      fns = []
                      for jj in range(nj):
                          fidx = 4 * db + jj
                          for cb in range(2):
                              fns.append(MM(bank(3 + cb), gmT3[:, fidx, :], bv[:, jj, cb * 512:(cb + 1) * 512],
                                            start=(fidx == 0), stop=(fidx == 21)))
                      P.mm(fns, reads=[b_gmT, b_blk], writes=[b_bank[3], b_bank[4]])
                  post_norm_add([bank(3), bank(4)], [b_bank[3], b_bank[4]], 1024)

                  chk(11)
                  blk, b_blk = wget(t, l, "ple")
                  bv = blk[:, 0:2048].rearrange("p (a b) -> p a b", a=2)
                  for cb in range(2):
                      P.mm([MM(bank(5 + cb), pTb[:, l, c, :], bv[:, c, cb * 512:(cb + 1) * 512], start=(c == 0), stop=(c == 1)) for c in range(2)],
                           reads=[b_pTb, b_blk], writes=[b_bank[5 + cb]])
                  P.op('act', ACTF(hn[:, :], xt[:, :], AF.Copy), reads=[b_x], writes=[b_hn])
                  P.mm([TR(bankb(0)[:, j * 128:(j + 1) * 128], hn[:, j * 128:(j + 1) * 128], identb) for j in range(8)],
                       reads=[b_hn, b_cstb], writes=[b_bank[0]])
                  P.op('dve', CP(hT[:, :, :], bankb(0)[:, :].rearrange("p (a b) -> p a b", a=8)), reads=[b_bank[0]], writes=[b_hT])
                  for cb in range(2):
                      proj_tok(t, l, f"pg{cb}", hT, b_hT, 1 + cb)
                      P.op('act', ACTF(sig[cb], bank(1 + cb), AF.Sigmoid), reads=[b_bank[1 + cb]], writes=[b_sig[cb]])
                      P.op('dve', TT(mixed[:, cb * 512:(cb + 1) * 512], bank(5 + cb), sig[cb], ALU.mult),
                           reads=[b_bank[5 + cb], b_sig[cb]], writes=[b_mixed])
                  post_norm_add([mixed[:, 0:512], mixed[:, 512:1024]], [b_mixed, b_mixed], 2048)
              P.dma('sp', DMA(out_d[tok0:tok0 + 128, :], xt[:, :]), b_x, reads=[b_x], writes=[Buf("o")])
        except _Stop:
            P.wait_all('sp', b_ring)
            P.dma('sp', DMA(out_d[0:128, :], xt[:, :]), b_x, reads=[b_x], writes=[Buf('o')])
        P.wait_all('sp', [b_x])
        P.emit()
    return nc


def _prep(inputs, S, L):
    blocks = [_layer_blocks(inputs, l) for l in range(L)]
    meta, off = [], 0
    for name, arr in blocks[0]:
        meta.append((name, off, arr.shape[1]))
        off += arr.shape[1]
    TOT = off
    wcat = np.stack([np.concatenate([a for _, a in blocks[l]], axis=1) for l in range(L)]).astype(np.float32)
    pfm = np.stack([_layer_pfm(inputs, l) for l in range(L)], axis=1).astype(np.float32)
    ptm = np.stack([_layer_ptm(inputs, l) for l in range(L)]).astype(np.float32)
    return meta, TOT, wcat, np.ascontiguousarray(pfm), ptm


def _run(inputs, core_ids, S, L, B):
    inputs = {k: np.asarray(v) for k, v in inputs.items()}
    meta, TOT, wcat, pfm, ptm = _prep(inputs, S, L)
    cst = _consts()
    import time as _t
    _t0 = _t.time()
    nc = build(S, L, meta, TOT)
    print('build_s', _t.time() - _t0, flush=True)
    in_maps = []
    for b in range(B):
        p = inputs['p'][:L, b]
        pT = np.ascontiguousarray(p.reshape(L, S, 2, 128).transpose(3, 0, 2, 1)).astype(np.float32)
        in_maps.append({
            "x": np.ascontiguousarray(inputs['x'][b]).astype(np.float32),
            "pT": pT,
            "pos": np.ascontiguousarray(inputs['positions'][b].reshape(S, 1)).astype(np.int32),
            "wcat": wcat, "pfm": pfm, "ptm": ptm, "cst": cst,
        })
    res = run_bass_kernel_spmd(nc, in_maps, core_ids=core_ids)
    return np.stack([res.results[b]["out"] for b in range(B)]).astype(np.float32)


def kernel(**inputs):
    return _run(inputs, [0, 1], 8192, NL, 2)
```
